# Optimizing a Trainium2 kernel written in Bass

```python
import math
import jax, jax.numpy as jnp
from jax import lax
import numpy as np

D_MODEL = 1024
BATCH = 8
SEQ = 4096
DEPTH = 1

CHUNK = 64
LEFT_CHUNKS = 8
BAND_CHUNKS = LEFT_CHUNKS + 1
BAND = BAND_CHUNKS * CHUNK

N_HEADS = 8
HEAD_DIM = 64
ATTN_WIDTH = N_HEADS * HEAD_DIM
MAX_REL = 128
N_REL = CHUNK + MAX_REL
ATTN_SCALE = 1.0 / math.sqrt(HEAD_DIM)
NEG_INF = -1e30

SSM_WIDTH = 512
SSM_GROUP = 16
SSM_GROUPS = SSM_WIDTH // SSM_GROUP
SSM_STATE = 64
DT_MIN = 0.001
DT_MAX = 0.1

NORM_EPS = 1e-6
IN_COLS = 4 * ATTN_WIDTH + 2 * SSM_WIDTH + 2 * D_MODEL
SPLITS = list(np.cumsum([ATTN_WIDTH, ATTN_WIDTH, ATTN_WIDTH, ATTN_WIDTH, SSM_WIDTH, SSM_WIDTH, D_MODEL])[:])

kernel_name = "hybrid_band_attn_s5_gated_block"


def rms_norm(x, gain):
    xf = x.astype(jnp.float32)
    inv = lax.rsqrt(jnp.mean(xf * xf, axis=-1, keepdims=True) + NORM_EPS)
    return (xf * inv * gain.astype(jnp.float32)).astype(x.dtype)


def band_attention(q, k, v, rel_bias):
    b, l, h, dh = q.shape
    nc = l // CHUNK
    qc = q.reshape(b, nc, CHUNK, h, dh)
    kc = k.reshape(b, nc, CHUNK, h, dh)
    vc = v.reshape(b, nc, CHUNK, h, dh)
    pad = ((0, 0), (LEFT_CHUNKS, 0), (0, 0), (0, 0), (0, 0))
    kp = jnp.pad(kc, pad)
    vp = jnp.pad(vc, pad)
    k_band = jnp.stack([kp[:, j:j + nc] for j in range(BAND_CHUNKS)], axis=2).reshape(b, nc, BAND, h, dh)
    v_band = jnp.stack([vp[:, j:j + nc] for j in range(BAND_CHUNKS)], axis=2).reshape(b, nc, BAND, h, dh)
    scores = jnp.einsum("bnqhd,bnkhd->bnhqk", qc, k_band,
                        preferred_element_type=jnp.float32) * ATTN_SCALE
    qi = jnp.arange(CHUNK)[:, None] + LEFT_CHUNKS * CHUNK
    kp_idx = jnp.arange(BAND)[None, :]
    rel = jnp.clip(qi - kp_idx, -(CHUNK - 1), MAX_REL) + (CHUNK - 1)
    bias = rel_bias.astype(jnp.float32)[:, rel]
    key_chunk = jnp.arange(nc)[:, None] - LEFT_CHUNKS + (jnp.arange(BAND) // CHUNK)[None, :]
    valid = (key_chunk >= 0)[None, :, None, None, :]
    scores = jnp.where(valid, scores + bias[None, None], NEG_INF)
    probs = jax.nn.softmax(scores, axis=-1).astype(v.dtype)
    out = jnp.einsum("bnhqk,bnkhd->bnqhd", probs, v_band)
    return out.reshape(b, l, h * dh)


def _complex_linear_combine(e1, e2):
    a1r, a1i, b1r, b1i = e1
    a2r, a2i, b2r, b2i = e2
    ar = a2r * a1r - a2i * a1i
    ai = a2r * a1i + a2i * a1r
    br = a2r * b1r - a2i * b1i + b2r
    bi = a2r * b1i + a2i * b1r + b2i
    return ar, ai, br, bi


def s5_ssm(u, a_re, a_im, log_dt, b_re, b_im, c_re, c_im, d_skip):
    bsz, l, _ = u.shape
    f32 = jnp.float32
    uf = u.astype(f32).reshape(bsz, l, SSM_GROUPS, SSM_GROUP)
    lam_re = a_re.astype(f32)
    lam_im = a_im.astype(f32)
    dt = jnp.exp(log_dt.astype(f32))[:, None]
    mag = jnp.exp(lam_re * dt)
    ab_re = mag * jnp.cos(lam_im * dt)
    ab_im = mag * jnp.sin(lam_im * dt)
    n_re = ab_re - 1.0
    n_im = ab_im
    den = lam_re * lam_re + lam_im * lam_im
    f_re = ((n_re * lam_re + n_im * lam_im) / den)[..., None]
    f_im = ((n_im * lam_re - n_re * lam_im) / den)[..., None]
    br_ = b_re.astype(f32)
    bi_ = b_im.astype(f32)
    bb_re = f_re * br_ - f_im * bi_
    bb_im = f_re * bi_ + f_im * br_
    bu_re = jnp.einsum("blgc,gpc->blgp", uf, bb_re)
    bu_im = jnp.einsum("blgc,gpc->blgp", uf, bb_im)
    a_r = jnp.broadcast_to(ab_re, bu_re.shape)
    a_i = jnp.broadcast_to(ab_im, bu_im.shape)
    _, _, h_re, h_im = lax.associative_scan(_complex_linear_combine, (a_r, a_i, bu_re, bu_im), axis=1)
    y = (jnp.einsum("blgp,gcp->blgc", h_re, c_re.astype(f32))
         - jnp.einsum("blgp,gcp->blgc", h_im, c_im.astype(f32)))
    y = y.reshape(bsz, l, SSM_WIDTH) + d_skip.astype(f32) * uf.reshape(bsz, l, SSM_WIDTH)
    return y.astype(u.dtype)


def setup_inputs(seed: int = 0) -> dict:
    key = jax.random.key(seed)
    ks = jax.random.split(key, 20)
    f32 = jnp.float32
    x = jax.random.normal(ks[0], (BATCH, SEQ, D_MODEL), f32)
    norm_gain = 1.0 + 0.02 * jax.random.normal(ks[1], (DEPTH, D_MODEL), f32)
    w_in = jax.random.normal(ks[2], (DEPTH, D_MODEL, IN_COLS), f32) * D_MODEL ** -0.5
    rel_bias = 0.1 * jax.random.normal(ks[3], (DEPTH, N_HEADS, N_REL), f32)
    n_idx = jnp.arange(SSM_STATE, dtype=f32)[None, None, :]
    ssm_a_re = -0.5 + 0.01 * jax.random.normal(ks[4], (DEPTH, SSM_GROUPS, SSM_STATE), f32)
    ssm_a_im = math.pi * n_idx + 0.01 * jax.random.normal(ks[5], (DEPTH, SSM_GROUPS, SSM_STATE), f32)
    ssm_log_dt = jax.random.uniform(ks[6], (DEPTH, SSM_GROUPS), f32,
                                    minval=math.log(DT_MIN), maxval=math.log(DT_MAX))
    b_scale = (2.0 * SSM_GROUP) ** -0.5
    ssm_b_re = jax.random.normal(ks[7], (DEPTH, SSM_GROUPS, SSM_STATE, SSM_GROUP), f32) * b_scale
    ssm_b_im = jax.random.normal(ks[8], (DEPTH, SSM_GROUPS, SSM_STATE, SSM_GROUP), f32) * b_scale
    c_scale = (2.0 * SSM_STATE) ** -0.5
    ssm_c_re = jax.random.normal(ks[9], (DEPTH, SSM_GROUPS, SSM_GROUP, SSM_STATE), f32) * c_scale
    ssm_c_im = jax.random.normal(ks[10], (DEPTH, SSM_GROUPS, SSM_GROUP, SSM_STATE), f32) * c_scale
    ssm_d = jax.random.normal(ks[11], (DEPTH, SSM_WIDTH), f32)
    w_glu = jax.random.normal(ks[12], (DEPTH, SSM_WIDTH, 2 * SSM_WIDTH), f32) * SSM_WIDTH ** -0.5
    b_glu = 0.01 * jax.random.normal(ks[13], (DEPTH, 2 * SSM_WIDTH), f32)
    w_attn_out = jax.random.normal(ks[14], (DEPTH, ATTN_WIDTH, D_MODEL), f32) * ATTN_WIDTH ** -0.5
    w_ssm_out = jax.random.normal(ks[15], (DEPTH, SSM_WIDTH, D_MODEL), f32) * SSM_WIDTH ** -0.5
    gate_bias = 0.01 * jax.random.normal(ks[16], (DEPTH, 2 * D_MODEL), f32)
    w_out = jax.random.normal(ks[17], (DEPTH, D_MODEL, D_MODEL), f32) * D_MODEL ** -0.5
    final_gain = 1.0 + 0.02 * jax.random.normal(ks[18], (D_MODEL,), f32)
    return {"x": x, "norm_gain": norm_gain, "w_in": w_in, "rel_bias": rel_bias,
            "ssm_a_re": ssm_a_re, "ssm_a_im": ssm_a_im, "ssm_log_dt": ssm_log_dt,
            "ssm_b_re": ssm_b_re, "ssm_b_im": ssm_b_im, "ssm_c_re": ssm_c_re, "ssm_c_im": ssm_c_im,
            "ssm_d": ssm_d, "w_glu": w_glu, "b_glu": b_glu, "w_attn_out": w_attn_out,
            "w_ssm_out": w_ssm_out, "gate_bias": gate_bias, "w_out": w_out, "final_gain": final_gain}


def reference(x, norm_gain, w_in, rel_bias, ssm_a_re, ssm_a_im, ssm_log_dt, ssm_b_re, ssm_b_im,
              ssm_c_re, ssm_c_im, ssm_d, w_glu, b_glu, w_attn_out, w_ssm_out, gate_bias, w_out,
              final_gain):
    bsz, l, _ = x.shape
    for layer in range(DEPTH):
        h = rms_norm(x, norm_gain[layer])
        proj = h @ w_in[layer]
        q, k, v, z_a, u, z_s, g_a, g_s = jnp.split(proj, SPLITS, axis=-1)
        g_a = g_a + gate_bias[layer, :D_MODEL]
        g_s = g_s + gate_bias[layer, D_MODEL:]
        heads = lambda t: t.reshape(bsz, l, N_HEADS, HEAD_DIM)
        y_a = band_attention(heads(q), heads(k), heads(v), rel_bias[layer])
        y_a = (y_a * jax.nn.silu(z_a)) @ w_attn_out[layer]
        y_s = s5_ssm(u, ssm_a_re[layer], ssm_a_im[layer], ssm_log_dt[layer], ssm_b_re[layer],
                     ssm_b_im[layer], ssm_c_re[layer], ssm_c_im[layer], ssm_d[layer])
        y_s = jax.nn.gelu(y_s)
        glu_a, glu_b = jnp.split(y_s @ w_glu[layer] + b_glu[layer], 2, axis=-1)
        y_s = glu_a * jax.nn.sigmoid(glu_b)
        y_s = (y_s * jax.nn.silu(z_s)) @ w_ssm_out[layer]
        merged = jax.nn.sigmoid(g_a) * y_a + jax.nn.sigmoid(g_s) * y_s
        x = x + merged @ w_out[layer]
    return rms_norm(x, final_gain)
```

```python
import contextlib
import math
import numpy as np
import concourse.bass as bass
import concourse.mybir as mybir
from concourse.bass_utils import run_bass_kernel_spmd

F32 = mybir.dt.float32
BF16 = mybir.dt.bfloat16
I32 = mybir.dt.int32
AF = mybir.ActivationFunctionType
ALU = mybir.AluOpType

L = 4096
D = 1024
TB = 512
NB = L // TB
NT = TB // 128
T = 8
NCB = TB // T
ATT_SCALE = 1.0 / 8.0
ENGS = ("pe", "act", "dve", "pool", "sp")


class Prog:
    def __init__(self, nc):
        self.nc = nc
        self.ops = []
        self.barriers = []
        self.sucnt = {}

    def op(self, eng, fn, reads=(), writes=(), dma=False, semkey=None, nobar=False):
        self.ops.append(dict(eng=eng, fn=fn, reads=tuple(reads), writes=tuple(writes), dma=dma, semkey=semkey, nobar=nobar))
        return len(self.ops) - 1

    def pe(self, fn, reads=(), writes=()):
        return self.op("pe", fn, reads, writes)

    def act(self, fn, reads=(), writes=()):
        return self.op("act", fn, reads, writes)

    def dve(self, fn, reads=(), writes=()):
        return self.op("dve", fn, reads, writes)

    def pool(self, fn, reads=(), writes=()):
        return self.op("pool", fn, reads, writes)

    def dma(self, fn, reads=(), writes=(), semkey=None, eng="sp", nobar=False):
        if not (semkey.startswith("xb") or semkey.startswith("wt") or semkey.startswith("w2_")):
            c = self.sucnt.get(eng, 0)
            self.sucnt[eng] = c + 1
            semkey = "%s_su%d" % (eng, c % 4)
        return self.op(eng, fn, reads, writes, dma=True, semkey=semkey, nobar=nobar)

    def barrier(self):
        self.barriers.append(len(self.ops))

    def build(self):
        nc = self.nc
        ops = self.ops
        n = len(ops)
        last_writer = {}
        readers = {}
        deps = [set() for _ in range(n)]
        for i, o in enumerate(ops):
            for r in o["reads"]:
                if r in last_writer:
                    deps[i].add(last_writer[r])
            for w in o["writes"]:
                if w in last_writer:
                    deps[i].add(last_writer[w])
                for rd in readers.get(w, ()):
                    deps[i].add(rd)
            for r in o["reads"]:
                readers.setdefault(r, []).append(i)
            for w in o["writes"]:
                last_writer[w] = i
                readers[w] = []
            deps[i].discard(i)
        last_dma = {}
        for i, o in enumerate(ops):
            if o["dma"]:
                if o["semkey"] in last_dma:
                    deps[i].add(last_dma[o["semkey"]])
                last_dma[o["semkey"]] = i
        for bidx in self.barriers:
            pre_last = {}
            pre_dmas = []
            for i in range(bidx):
                o = ops[i]
                if o["dma"]:
                    if not o["nobar"]:
                        pre_dmas.append(i)
                else:
                    pre_last[o["eng"]] = i
            seen = set()
            for i in range(bidx, n):
                e = ops[i]["eng"]
                if e in seen:
                    continue
                seen.add(e)
                for v in pre_last.values():
                    deps[i].add(v)
                for v in pre_dmas:
                    deps[i].add(v)
                if len(seen) == len(ENGS):
                    break
        for i, o in enumerate(ops):
            rm = set()
            for d in deps[i]:
                p = ops[d]
                if p["dma"]:
                    continue
                if p["eng"] == o["eng"] and o["eng"] == "pe" and not o["dma"]:
                    rm.add(d)
            deps[i] -= rm
        needed = set()
        for i in range(n):
            needed |= deps[i]
        es = contextlib.ExitStack()
        CAP = 16000
        sems = {}
        gen = {}
        cnt = {}
        tick = {}

        def next_tick(base, inc):
            g = gen.get(base, 0)
            c = cnt.get((base, g), 0)
            if c + inc > CAP:
                g += 1
                gen[base] = g
                c = 0
            c += inc
            cnt[(base, g)] = c
            key = (base, g)
            if key not in sems:
                sems[key] = es.enter_context(nc.semaphore("s%d" % len(sems)))
            return key, c

        for i, o in enumerate(ops):
            if o["dma"]:
                tick[i] = next_tick(("dma", o["semkey"]), 16)
            elif i in needed:
                tick[i] = next_tick(("eng", o["eng"]), 1)
        per_eng = {e: [] for e in ENGS}
        for i, o in enumerate(ops):
            per_eng[o["eng"]].append(i)
        engobj = {"pe": "tensor", "act": "scalar", "dve": "vector", "pool": "gpsimd", "sp": "sync"}
        final_ticks = {}
        for i, o in enumerate(ops):
            if o["dma"]:
                final_ticks[tick[i][0]] = tick[i][1]
        with nc.Block() as block:
            for e in ENGS:
                idxs = per_eng[e]

                def body(eng, idxs=idxs, e=e):
                    waited = {}
                    for i in idxs:
                        o = ops[i]
                        for d in sorted(deps[i]):
                            k, v = tick[d]
                            if waited.get(k, 0) >= v:
                                continue
                            eng.wait_ge(sems[k], v)
                            waited[k] = v
                        ins = o["fn"](eng)
                        if i in tick:
                            k, v = tick[i]
                            ins.then_inc(sems[k], 16 if o["dma"] else 1)
                    if e == "sp":
                        for k, v in final_ticks.items():
                            if waited.get(k, 0) < v:
                                eng.wait_ge(sems[k], v)

                getattr(block, engobj[e])(body)
        es.close()
        return {e: len(per_eng[e]) for e in ENGS}


class _Stop(Exception):
    pass


def build_program(nblocks=NB, stage=99, maxops=None, marks=None):
    nc = bass.Bass("TRN2", target_bir_lowering=False)
    P = Prog(nc)

    def din(name, shape, dt=F32):
        return nc.dram_tensor(name, list(shape), dt, kind="ExternalInput")

    x_t = din("x", [L, D])
    ng_t = din("norm_gain", [D])
    win_t = din("w_in", [D, 5120])
    rb_t = din("rel_bias", [8, 192])
    are_t = din("ssm_a_re", [32, 64])
    aim_t = din("ssm_a_im", [32, 64])
    ldt_t = din("ssm_log_dt", [32])
    bre_t = din("ssm_b_re", [32, 64, 16])
    bim_t = din("ssm_b_im", [32, 64, 16])
    cre_t = din("ssm_c_re", [32, 16, 64])
    cim_t = din("ssm_c_im", [32, 16, 64])
    sd_t = din("ssm_d", [512])
    wglu_t = din("w_glu", [512, 1024])
    bglu_t = din("b_glu", [1024])
    wao_t = din("w_attn_out", [512, 1024])
    wso_t = din("w_ssm_out", [512, 1024])
    gb_t = din("gate_bias", [2048])
    wout_t = din("w_out", [D, D])
    fg_t = din("final_gain", [D])
    ident_t = din("c_ident", [128, 128])
    mask_t = din("c_maskbd", [128, 128])
    flip_t = din("c_flip", [128, 128])
    vecs_t = din("p_vecs", [128, 92])
    bT_t = din("p_bT", [128, 2, 16, 32])
    cT_t = din("p_cT", [128, 2, 16, 32])
    out_t = nc.dram_tensor("out", [L, D], F32, kind="ExternalOutput")
    wsc_t = nc.dram_tensor("wsc", [40, 128, 8, 128], BF16, kind="Internal")
    wsc2_t = nc.dram_tensor("wsc2", [4, 128, 8, 256], BF16, kind="Internal")
    rbx_t = din("rel_bias_x", [8, 384])
    x_d, out_d, win_d = x_t.ap(), out_t.ap(), win_t.ap()
    wsc_d, wsc2_d, rbx_d = wsc_t.ap(), wsc2_t.ap(), rbx_t.ap()

    es = contextlib.ExitStack()

    def sb(name, shape, dt=F32, stack=None):
        return (stack or es).enter_context(nc.sbuf_tensor(name, list(shape), dt))

    def psum(name, shape, dt=F32):
        return es.enter_context(nc.psum_tensor(name, list(shape), dt))

    NCg = nc.allow_non_contiguous_dma(reason="small setup loads")
    NCg.__enter__()

    Wglu = sb("Wglu", [128, 4, 1024], BF16)
    Wao = sb("Wao", [128, 4, 1024], BF16)
    Wso = sb("Wso", [128, 4, 1024], BF16)
    Wout = sb("Wout", [128, 8, 1024], BF16)
    Wa = sb("Wa", [128, 4, T, 2, 128], BF16)
    Wi = sb("Wi", [128, 16, T, 2, 32], BF16)
    Kin = sb("Kin", [128, 4, T, 128], BF16)
    CS = sb("CS", [128, 16, NCB])
    SN = sb("SN", [128, 16, NCB])
    rT = sb("rT", [128, 16])
    Xl = sb("Xl", [128, 2, 16])
    Xh = sb("Xh", [128, 2, 16, NCB + 1], BF16)
    identb = sb("identb", [128, 128], BF16)
    vecs = sb("vecs", [128, 92])
    g1T, gbT, bgT, cb = vecs[:, 0:8], vecs[:, 8:24], vecs[:, 24:32], vecs[:, 84:92]
    fgb = sb("fgb", [128, 1024])
    EB = sb("EB", [128, 8, 2, 128], BF16)
    mhalf = sb("mhalf", [128, 1])
    PB = [sb("PB%d" % s, [128, 5, 128], BF16) for s in range(3)]

    MM = [psum("mm%d" % i, [128, 512]) for i in range(2)]
    SCB = [psum("sc%d" % i, [128, 4, 128]) for i in range(4)]
    PV = [psum("pv%d" % i, [128, 4, 128]) for i in range(2)]
    BK = [MM[0][:, :], MM[1][:, :]] + [SCB[i][:, :, :].rearrange("p a b -> p (a b)") for i in range(4)] + \
         [PV[i][:, :, :].rearrange("p a b -> p (a b)") for i in range(2)]
    BKN = ["mm0", "mm1", "sc0", "sc1", "sc2", "sc3", "pv0", "pv1"]
    ring = {"n": 8, "c": 0, "held": set()}

    def mm_next():
        for _ in range(16):
            i = ring["c"] % ring["n"]
            ring["c"] += 1
            if i not in ring["held"]:
                return i
        raise RuntimeError("no free PSUM bank in ring")

    def bn(i):
        return BKN[i]

    ss = contextlib.ExitStack()
    identf = sb("identf", [128, 128], F32, ss)
    maskbd = sb("maskbd", [128, 128], F32, ss)
    P.dma(lambda e: e.dma_start(out=identf[:], in_=ident_t.ap()), writes=["identf"], semkey="c0")
    P.dma(lambda e: e.dma_start(out=maskbd[:], in_=mask_t.ap()), writes=["maskbd"], semkey="c1")
    P.dve(lambda e: e.tensor_copy(identb[:], identf[:]), reads=["identf"], writes=["identb"])
    P.dve(lambda e: e.memset(mhalf[:], -0.5), writes=["mhalf"])
    P.dma(lambda e: e.dma_start(out=vecs[:], in_=vecs_t.ap()),
          writes=["g1T", "gbT", "bgT", "cb", "lre", "lim", "ldt0", "ldt1", "dT"], semkey="c2")
    P.dma(lambda e: e.dma_start(out=fgb[:], in_=fg_t.ap().rearrange("(o d) -> o d", o=1).partition_broadcast(128)),
          writes=["fgb"], semkey="c5")
    if marks is not None:
        marks['a_consts'] = len(P.ops)
    win_v = win_d.rearrange("(kt p) c -> p kt c", p=128)
    ct_order = list(range(0, 8)) + ["v", "za"] + list(range(16, 40))

    def cast_ct(ct):
        P.dma(lambda e, ct=ct: e.dma_start(out=wsc_d[ct], in_=win_v[:, :, ct * 128:(ct + 1) * 128]),
              writes=["wsc%d" % ct], semkey="wc%d" % (ct % 4), eng="pool", nobar=True)

    def cast_w2(idx):
        c0 = 1024 + idx * 256
        P.dma(lambda e, idx=idx, c0=c0: e.dma_start(out=wsc2_d[idx], in_=win_v[:, :, c0:c0 + 256]),
              writes=["wsc2_%d" % idx], semkey="w2c%d" % (idx % 2), eng="pool", nobar=True)

    for ct in ct_order:
        if ct == "v":
            cast_w2(0), cast_w2(1)
        elif ct == "za":
            cast_w2(2), cast_w2(3)
        else:
            cast_ct(ct)
    if marks is not None:
        marks['b_wsc'] = len(P.ops)
    for nm, wt_, src, nk in (("Wglu", Wglu, wglu_t, 4), ("Wao", Wao, wao_t, 4), ("Wso", Wso, wso_t, 4), ("Wout", Wout, wout_t, 8)):
        sv = src.ap().rearrange("(kt p) c -> p kt c", p=128)
        for kt in range(nk):
            P.dma(lambda e, wt_=wt_, sv=sv, kt=kt: e.dma_start(out=wt_[:, kt, :], in_=sv[:, kt, :]),
                  writes=[nm], semkey="wr%d" % (kt % 2), eng="pool", nobar=True)

    if marks is not None:
        marks['c_resw'] = len(P.ops)
    ebraw = sb("ebraw", [128, 8, 2, 128], F32, ss)
    flipf = sb("flipf", [128, 128], F32, ss)
    P.dma(lambda e: e.dma_start(out=flipf[:], in_=flip_t.ap()), writes=["flipf"], semkey="c9")
    P.dve(lambda e: e.memset(ebraw[:], 0.0), writes=["ebraw"])
    for h in range(8):
        base = h * 384
        P.dma(lambda e, h=h, base=base: e.dma_start(out=ebraw[:, h, 1, :], in_=bass.AP(rbx_t, base + 64, [[1, 128], [1, 128]])),
              reads=["ebraw"], writes=["ebraw%d_1" % h], semkey="eb%d" % (h % 4))
        P.dma(lambda e, h=h, base=base: e.dma_start(out=ebraw[64:128, h, 0, :], in_=bass.AP(rbx_t, base, [[1, 64], [1, 128]])),
              reads=["ebraw"], writes=["ebraw%d_0a" % h], semkey="eba%d" % (h % 4))
        P.dma(lambda e, h=h, base=base: e.dma_start(out=ebraw[0:64, h, 0, 64:128], in_=bass.AP(rbx_t, base, [[1, 64], [1, 64]])),
              reads=["ebraw"], writes=["ebraw%d_0b" % h], semkey="ebb%d" % (h % 4))
    allraw = [w for o in P.ops for w in o["writes"] if w.startswith("ebraw")]
    if marks is not None:
        marks['d_bias'] = len(P.ops)
    def st(name, shape, dt=F32):
        return sb(name, shape, dt, ss)

    lre, lim, ldt, dT = vecs[:, 36:52], vecs[:, 52:68], vecs[:, 68:84], vecs[:, 32:36]
    bTx = st("bTx", [128, 2, 16, 32]); cTx = st("cTx", [128, 2, 16, 32])
    P.dma(lambda e: e.dma_start(out=bTx[:], in_=bT_t.ap()), writes=["bTr", "bTr_0", "bTr_1", "bTi", "bTi_0", "bTi_1"], semkey="s3")
    P.dma(lambda e: e.dma_start(out=cTx[:], in_=cT_t.ap()), writes=["cblkrT", "cblkiT"], semkey="s4")
    bTr, bTi, cTr, cTi = bTx[:, 0], bTx[:, 1], cTx[:, 0], cTx[:, 1]
    if marks is not None:
        marks['e_ssmloads'] = len(P.ops)
    RCT = ["cblkrT", "cblkiT"]

    def tt_(eng, out, a, b, op, reads, writes):
        getattr(P, eng)(lambda e: e.tensor_tensor(out=out, in0=a, in1=b, op=op), reads=reads, writes=writes)

    if marks is not None:
        marks['f_ctrans'] = len(P.ops)
    names = ["dtv", "th", "lr", "mag", "kq", "kf", "thr", "thc", "sn", "cs", "abr", "abi", "nr", "den", "rden",
             "fr", "fi", "t1", "t2", "t3"]
    V = {nm: st("v_" + nm, [128, 16]) for nm in names}
    ki = st("v_ki", [128, 16], I32)
    SA = ["lre", "lim", "ldt0", "ldt1"]
    P.act(lambda e: e.activation(out=V["dtv"][:], in_=ldt[:], func=AF.Exp), reads=SA, writes=["dtv"])
    tt_("dve", V["th"][:], lim[:], V["dtv"][:], ALU.mult, SA + ["dtv"], ["th"])
    tt_("dve", V["lr"][:], lre[:], V["dtv"][:], ALU.mult, SA + ["dtv"], ["lr"])
    P.act(lambda e: e.activation(out=V["mag"][:], in_=V["lr"][:], func=AF.Exp), reads=["lr"], writes=["mag"])
    P.act(lambda e: e.activation(out=rT[:], in_=V["lr"][:], func=AF.Exp, scale=float(T)), reads=["lr"], writes=["rT"])
    TWO_PI = 2.0 * math.pi

    def range_reduce(dst, src_name, shift):
        P.dve(lambda e: e.tensor_scalar(out=V["kq"][:], in0=V[src_name][:], scalar1=shift, scalar2=1.0 / TWO_PI,
                                        op0=ALU.add, op1=ALU.mult), reads=[src_name], writes=["kq"])
        P.dve(lambda e: e.tensor_copy(ki[:], V["kq"][:]), reads=["kq"], writes=["ki"])
        P.dve(lambda e: e.tensor_copy(V["kf"][:], ki[:]), reads=["ki"], writes=["kf"])
        P.dve(lambda e: e.tensor_scalar(out=V["kq"][:], in0=V[src_name][:], scalar1=shift, scalar2=None, op0=ALU.add),
              reads=[src_name, "kf"], writes=["kq"])
        P.dve(lambda e: e.scalar_tensor_tensor(out=V[dst][:], in0=V["kf"][:], scalar=-TWO_PI, in1=V["kq"][:],
                                               op0=ALU.mult, op1=ALU.add), reads=["kf", "kq"], writes=[dst])

    range_reduce("thr", "th", 0.0)
    P.act(lambda e: e.activation(out=V["sn"][:], in_=V["thr"][:], func=AF.Sin), reads=["thr"], writes=["sn"])
    range_reduce("thc", "th", math.pi / 2.0)
    P.act(lambda e: e.activation(out=V["cs"][:], in_=V["thc"][:], func=AF.Sin), reads=["thc"], writes=["cs"])
    tt_("dve", V["abr"][:], V["mag"][:], V["cs"][:], ALU.mult, ["mag", "cs"], ["abr"])
    tt_("dve", V["abi"][:], V["mag"][:], V["sn"][:], ALU.mult, ["mag", "sn"], ["abi"])
    P.dve(lambda e: e.tensor_scalar(out=V["nr"][:], in0=V["abr"][:], scalar1=-1.0, scalar2=None, op0=ALU.add),
          reads=["abr"], writes=["nr"])
    tt_("dve", V["t1"][:], lre[:], lre[:], ALU.mult, SA, ["t1"])
    tt_("dve", V["t2"][:], lim[:], lim[:], ALU.mult, SA, ["t2"])
    tt_("dve", V["den"][:], V["t1"][:], V["t2"][:], ALU.add, ["t1", "t2"], ["den"])
    P.dve(lambda e: e.reciprocal(V["rden"][:], V["den"][:]), reads=["den"], writes=["rden"])
    tt_("dve", V["t1"][:], V["nr"][:], lre[:], ALU.mult, ["nr"] + SA, ["t1"])
    tt_("dve", V["t2"][:], V["abi"][:], lim[:], ALU.mult, ["abi"] + SA, ["t2"])
    tt_("dve", V["t3"][:], V["t1"][:], V["t2"][:], ALU.add, ["t1", "t2"], ["t3"])
    tt_("dve", V["fr"][:], V["t3"][:], V["rden"][:], ALU.mult, ["t3", "rden"], ["fr"])
    tt_("dve", V["t1"][:], V["abi"][:], lre[:], ALU.mult, ["abi"] + SA, ["t1"])
    tt_("dve", V["t2"][:], V["nr"][:], lim[:], ALU.mult, ["nr"] + SA, ["t2"])
    tt_("dve", V["t3"][:], V["t1"][:], V["t2"][:], ALU.subtract, ["t1", "t2"], ["t3"])
    tt_("dve", V["fi"][:], V["t3"][:], V["rden"][:], ALU.mult, ["t3", "rden"], ["fi"])
    if marks is not None:
        marks['g_scal'] = len(P.ops)
    Pr = st("Pr", [128, T + 1, 16]); Pi = st("Pi", [128, T + 1, 16])
    P.dve(lambda e: e.memset(Pr[:, 0, :], 1.0), writes=["Pr0"])
    P.dve(lambda e: e.memset(Pi[:, 0, :], 0.0), writes=["Pi0"])
    for k in range(T):
        a, b_ = "Pr%d" % k, "Pi%d" % k
        tt_("dve", V["t1"][:], Pr[:, k, :], V["abr"][:], ALU.mult, [a, "abr"], ["t1"])
        tt_("dve", V["t2"][:], Pi[:, k, :], V["abi"][:], ALU.mult, [b_, "abi"], ["t2"])
        tt_("dve", Pr[:, k + 1, :], V["t1"][:], V["t2"][:], ALU.subtract, ["t1", "t2"], ["Pr%d" % (k + 1)])
        tt_("dve", V["t1"][:], Pr[:, k, :], V["abi"][:], ALU.mult, [a, "abi"], ["t1"])
        tt_("dve", V["t2"][:], Pi[:, k, :], V["abr"][:], ALU.mult, [b_, "abr"], ["t2"])
        tt_("dve", Pi[:, k + 1, :], V["t1"][:], V["t2"][:], ALU.add, ["t1", "t2"], ["Pi%d" % (k + 1)])
    P.dve(lambda e: e.reciprocal(V["t3"][:], rT[:]), reads=["rT"], writes=["t3"])
    tt_("dve", CS[:, :, 0], Pr[:, T, :], V["t3"][:], ALU.mult, ["Pr%d" % T, "t3"], ["CS"])
    tt_("dve", SN[:, :, 0], Pi[:, T, :], V["t3"][:], ALU.mult, ["Pi%d" % T, "t3"], ["SN"])
    tb1 = st("tb1", [128, 16, 32]); tb2 = st("tb2", [128, 16, 32])
    w = 1
    while w < NCB:
        cbr = CS[:, :, w - 1:w].to_broadcast([128, 16, w])
        sbr = SN[:, :, w - 1:w].to_broadcast([128, 16, w])
        src_c, src_s = CS[:, :, 0:w], SN[:, :, 0:w]
        t1v, t2v = tb1[:, :, 0:w], tb2[:, :, 0:w]
        tt_("dve", t1v, src_c, cbr, ALU.mult, ["CS", "SN"], ["tb1"])
        tt_("dve", t2v, src_s, sbr, ALU.mult, ["CS", "SN"], ["tb2"])
        tt_("dve", t1v, t1v, t2v, ALU.subtract, ["tb1", "tb2"], ["tb1"])
        tt_("dve", t2v, src_c, sbr, ALU.mult, ["CS", "SN", "tb1"], ["tb2"])
        tt_("dve", CS[:, :, w:2 * w], t1v, t1v, ALU.max, ["tb1", "tb2"], ["CS"])
        tt_("dve", t1v, src_s, cbr, ALU.mult, ["CS", "SN"], ["tb1"])
        tt_("dve", SN[:, :, w:2 * w], t1v, t2v, ALU.add, ["tb1", "tb2"], ["SN"])
        w *= 2
    if marks is not None:
        marks['h_tables'] = len(P.ops)
    bbr = st("bbr", [128, 16, 32]); bbi = st("bbi", [128, 16, 32])
    u1 = st("u1", [128, 16, 32]); u2 = st("u2", [128, 16, 32])
    BR = ["bTr", "bTr_0", "bTr_1", "bTi", "bTi_0", "bTi_1"]

    def bc(v_):
        return v_.unsqueeze(2).to_broadcast([128, 16, 32])

    tt_("dve", u1[:], bTr[:], bc(V["fr"][:]), ALU.mult, BR + ["fr"], ["u1"])
    tt_("dve", u2[:], bTi[:], bc(V["fi"][:]), ALU.mult, BR + ["fi"], ["u2"])
    tt_("dve", bbr[:], u1[:], u2[:], ALU.subtract, ["u1", "u2"], ["bbr"])
    tt_("dve", u1[:], bTi[:], bc(V["fr"][:]), ALU.mult, BR + ["fr"], ["u1"])
    tt_("dve", u2[:], bTr[:], bc(V["fi"][:]), ALU.mult, BR + ["fi"], ["u2"])
    tt_("dve", bbi[:], u1[:], u2[:], ALU.add, ["u1", "u2"], ["bbi"])
    BpTr = st("BpTr", [128, T, 16, 32], BF16); BpTi = st("BpTi", [128, T, 16, 32], BF16)
    for k in range(T):
        pr, pi = bc(Pr[:, k, :]), bc(Pi[:, k, :])
        tt_("dve", u1[:], bbr[:], pr, ALU.mult, ["bbr", "Pr%d" % k], ["u1"])
        tt_("dve", u2[:], bbi[:], pi, ALU.mult, ["bbi", "Pi%d" % k], ["u2"])
        tt_("dve", BpTr[:, k, :, :], u1[:], u2[:], ALU.subtract, ["u1", "u2"], ["BpTr%d" % k])
        tt_("dve", u1[:], bbr[:], pi, ALU.mult, ["bbr", "Pi%d" % k], ["u1"])
        tt_("dve", u2[:], bbi[:], pr, ALU.mult, ["bbi", "Pr%d" % k], ["u2"])
        tt_("dve", BpTi[:, k, :, :], u1[:], u2[:], ALU.add, ["u1", "u2"], ["BpTi%d" % k])
    if marks is not None:
        marks['i_bpt'] = len(P.ops)
    for kq in range(4):
        for reim, src in ((0, BpTr), (1, BpTi)):
            bi = mm_next()
            tpv = BK[bi].bitcast(BF16)[:, 0:T * 128].rearrange("p (s c) -> p s c", c=128)
            for s in range(T):
                k = T - 1 - s
                P.pe(lambda e, tpv=tpv, s=s, src=src, k=k, kq=kq: e.transpose(
                    tpv[:, s, :], src[:, k, 4 * kq:4 * kq + 4, :].rearrange("p a b -> p (a b)"), identb[:]),
                    reads=["BpT%s%d" % ("ri"[reim], k), "identb"], writes=[bn(bi)])
            P.act(lambda e, tpv=tpv, kq=kq, reim=reim: e.activation(out=Wa[:, kq, :, reim, :], in_=tpv, func=AF.Copy),
                  reads=[bn(bi)], writes=["Wa"])
    if marks is not None:
        marks['j_wa'] = len(P.ops)
    for j in range(T):
        pr, pi = bc(Pr[:, j + 1, :]), bc(Pi[:, j + 1, :])
        rd = RCT + ["Pr%d" % (j + 1), "Pi%d" % (j + 1)]
        tt_("dve", u1[:], cTr[:], pr, ALU.mult, rd, ["u1"])
        tt_("dve", u2[:], cTi[:], pi, ALU.mult, rd, ["u2"])
        tt_("dve", Wi[:, :, j, 0, :], u1[:], u2[:], ALU.subtract, ["u1", "u2"], ["Wi"])
        tt_("dve", u1[:], cTr[:], pi, ALU.mult, rd, ["u1"])
        tt_("dve", u2[:], cTi[:], pr, ALU.mult, rd, ["u2"])
        P.dve(lambda e, j=j: e.scalar_tensor_tensor(out=Wi[:, :, j, 1, :], in0=u1[:], scalar=-1.0, in1=u2[:],
                                                    op0=ALU.mult, op1=ALU.subtract), reads=["u1", "u2"], writes=["Wi"])
    if marks is not None:
        marks['k_wi'] = len(P.ops)
    cTrb = st("cTrb", [128, 16, 32], BF16); ncTib = st("ncTib", [128, 16, 32], BF16)
    P.dve(lambda e: e.tensor_copy(cTrb[:], cTr[:]), reads=RCT, writes=["cTrb"])
    P.dve(lambda e: e.tensor_scalar(out=ncTib[:], in0=cTi[:], scalar1=-1.0, scalar2=None, op0=ALU.mult),
          reads=RCT, writes=["ncTib"])
    kt1 = st("kt1", [128, 128])
    for kq in range(4):
        for tau in range(T):
            bi = mm_next()
            ov = BK[bi][:, 0:128]
            sl = slice(4 * kq, 4 * kq + 4)
            P.pe(lambda e, ov=ov, tau=tau, sl=sl: e.matmul(
                ov, lhsT=BpTr[:, tau, sl, :].rearrange("p a b -> p (a b)"),
                rhs=cTrb[:, sl, :].rearrange("p a b -> p (a b)"), start=True, stop=False),
                reads=["BpTr%d" % tau, "cTrb"], writes=[bn(bi)])
            P.pe(lambda e, ov=ov, tau=tau, sl=sl: e.matmul(
                ov, lhsT=BpTi[:, tau, sl, :].rearrange("p a b -> p (a b)"),
                rhs=ncTib[:, sl, :].rearrange("p a b -> p (a b)"), start=False, stop=True),
                reads=["BpTi%d" % tau, "ncTib"], writes=[bn(bi)])
            if tau == 0:
                tt_("dve", kt1[:], ov, maskbd[:], ALU.mult, [bn(bi), "maskbd"], ["kt1"])
                P.dve(lambda e, kq=kq: e.scalar_tensor_tensor(out=Kin[:, kq, 0, :], in0=identf[:], scalar=dT[:, kq:kq + 1],
                                                             in1=kt1[:], op0=ALU.mult, op1=ALU.add),
                      reads=["kt1", "identf", "dT"], writes=["Kin"])
            else:
                tt_("dve", Kin[:, kq, tau, :], ov, maskbd[:], ALU.mult, [bn(bi), "maskbd"], ["Kin"])
    for h0 in range(0, 8, 2):
        bi = mm_next()
        for k_ in range(4):
            hh, dd = h0 + k_ // 2, k_ % 2
            P.pe(lambda e, bi=bi, k_=k_, hh=hh, dd=dd: e.matmul(BK[bi][:, k_ * 128:(k_ + 1) * 128], lhsT=flipf[:], rhs=ebraw[:, hh, dd, :],
                                                              start=True, stop=True), reads=allraw + ["flipf"], writes=[bn(bi)])
        P.act(lambda e, bi=bi, h0=h0: e.activation(out=EB[:, h0:h0 + 2, :, :].rearrange("p a b c -> p (a b c)"), in_=BK[bi],
                                                   func=AF.Exp), reads=[bn(bi)], writes=["EB"])
    P.dve(lambda e: e.memset(EB[64:128, :, 0, 0:64], 0.0), reads=["EB"], writes=["EB"])

    P.dve(lambda e: e.memset(Xl[:], 0.0), writes=["Xl"])
    P.dve(lambda e: e.memset(Xh[:], 0.0), writes=["Xh"])
    if marks is not None:
        marks['l_kin'] = len(P.ops)
    ss.close()
    P.barrier()

    xbuf = [sb("xbuf%d" % i, [128, 1024]) for i in range(3)]
    hTs = [sb("hT%d" % i, [128, 8, TB], BF16) for i in range(2)]
    qT = sb("qT", [128, 4, TB], BF16)
    kT = sb("kT", [128, 4, 2 * TB], BF16)
    Vr = sb("Vr", [128, 8, 8, 65], BF16)
    zas = sb("zas", [128, NT, 512], BF16)
    uT = sb("uT", [128, 4, TB], BF16)
    zss = sb("zss", [128, 4, TB], BF16)
    yaT = sb("yaT", [128, 4, TB], BF16)
    yat = sb("yat", [128, 512], BF16)
    mrg = sb("mrg", [128, 8, TB], BF16)
    WT = [sb("wt%d" % i, [128, 8, 128], BF16) for i in range(4)]
    W2 = [sb("w2_%d" % i, [128, 8, 256], BF16) for i in range(2)]
    tmpE = [sb("tmpE%d" % i, [128, 2, 128]) for i in range(2)]
    Sh = [sb("Sh%d" % i, [128, 4, NCB]) for i in range(2)]
    Gi = [sb("Gi%d" % i, [128, 4, NCB]) for i in range(2)]
    Gt = sb("Gt", [128, 4, NCB])
    E0 = sb("E0", [128, 512]); E1 = sb("E1", [128, 512])
    SG = [sb("SG%d" % i, [128, 512], BF16) for i in range(2)]
    ssq = sb("ssq", [128, 8]); rstd = sb("rstd", [128, 8]); rec = sb("rec", [128, 8])
    xnb = [E0[:, :].bitcast(BF16), E1[:, :].bitcast(BF16)]
    xnres = ["E0", "E1"]
    b1x = {}
    P.dve(lambda e: e.memset(Vr[:, :, :, 64:65], 1.0), writes=["Vr_ones"])

    wtc = [0]
    w2c = [0]
    scc = [0]

    def stream_ct(ct):
        i = wtc[0] % 4
        wtc[0] += 1
        P.dma(lambda e, i=i, ct=ct: e.dma_start(out=WT[i][:], in_=wsc_d[ct]), reads=["wsc%d" % ct], writes=["wt%d" % i],
              semkey="wt%d" % i)
        return i

    def stream_w2(idx):
        i = w2c[0] % 2
        w2c[0] += 1
        P.dma(lambda e, i=i, idx=idx: e.dma_start(out=W2[i][:], in_=wsc2_d[idx]), reads=["wsc2_%d" % idx],
              writes=["w2_%d" % i], semkey="w2_%d" % i)
        return i

    def proj_fm(ct):
        wi_ = stream_ct(ct)
        bi = mm_next()
        for kt in range(8):
            P.pe(lambda e, hT=hT, bi=bi, wi_=wi_, kt=kt: e.matmul(BK[bi], lhsT=WT[wi_][:, kt, :], rhs=hT[:, kt, :],
                                                          start=(kt == 0), stop=(kt == 7)),
                 reads=["wt%d" % wi_, HT], writes=[bn(bi)])
        return bi

    xc = [0]

    def load_x(gt):
        i = xc[0] % 3
        xc[0] += 1
        P.dma(lambda e, i=i, gt=gt: e.dma_start(out=xbuf[i][:], in_=x_d[gt * 128:(gt + 1) * 128, :]),
              writes=["xbuf%d" % i], semkey="xb%d" % i)
        return i

    def chk(n_):
        if stage == n_:
            raise _Stop()

    try:
      for b in range(nblocks):
        chk(0)
        def b1_load(bb, t):
            b1x[(bb, t)] = load_x(bb * NT + t)

        def b1_act(bb, t):
            b1_sq(bb, t)
            b1_scale(bb, t)

        def b1_sq(bb, t):
            xi = b1x[(bb, t)]
            xw, xres = xnb[t % 2], xnres[t % 2]
            P.act(lambda e, xi=xi, t=t, xw=xw: e.activation(out=xw, in_=xbuf[xi][:], func=AF.Square, accum_out=ssq[:, t:t + 1]),
                  reads=["xbuf%d" % xi], writes=[xres, "ssq%d" % t])
            P.pool(lambda e, t=t: e.tensor_scalar(out=rstd[:, t:t + 1], in0=ssq[:, t:t + 1], scalar1=1.0 / D, scalar2=1e-6,
                                                  op0=ALU.mult, op1=ALU.add), reads=["ssq%d" % t], writes=["rstd%d" % t])
            P.pool(lambda e, t=t: e.tensor_tensor(out=rstd[:, t:t + 1], in0=rstd[:, t:t + 1], in1=mhalf[:], op=ALU.pow),
                   reads=["rstd%d" % t, "mhalf"], writes=["rstd%d" % t])

        def b1_scale(bb, t):
            xi = b1x[(bb, t)]
            xw, xres = xnb[t % 2], xnres[t % 2]
            P.act(lambda e, xi=xi, t=t, xw=xw: e.activation(out=xw, in_=xbuf[xi][:], func=AF.Copy, scale=rstd[:, t:t + 1]),
                  reads=["xbuf%d" % xi, "rstd%d" % t], writes=[xres])

        def b1_pe(bb, t):
            hTw = hTs[bb % 2]
            hres = "hT%d" % (bb % 2)
            xw, xres = xnb[t % 2], xnres[t % 2]
            bi = mm_next()
            tpv = BK[bi].bitcast(BF16).rearrange("p (k c) -> p k c", c=128)
            for kt in range(8):
                P.pe(lambda e, tpv=tpv, kt=kt, xw=xw: e.transpose(tpv[:, kt, :], xw[:, kt * 128:(kt + 1) * 128], identb[:]),
                     reads=[xres, "identb"], writes=[bn(bi)])
            P.dve(lambda e, tpv=tpv, t=t, hTw=hTw: e.tensor_tensor(out=hTw[:, :, t * 128:(t + 1) * 128], in0=tpv,
                                                                  in1=g1T[:, :].unsqueeze(2).to_broadcast([128, 8, 128]), op=ALU.mult),
                  reads=[bn(bi), "g1T"], writes=[hres])

        if b == 0:
            for t in range(NT):
                b1_load(0, t)
                b1_act(0, t)
                b1_pe(0, t)
        hT = hTs[b % 2]
        HT = "hT%d" % (b % 2)
        chk(1)
        kofs = (b % 2) * TB
        for c in range(4):
            bi = proj_fm(c)
            P.act(lambda e, bi=bi, c=c: e.activation(out=qT[:, c, :], in_=BK[bi], func=AF.Copy),
                  reads=[bn(bi)], writes=["qT"])
        for c in range(4):
            bi = proj_fm(4 + c)
            P.dve(lambda e, bi=bi, c=c, kofs=kofs: e.tensor_copy(kT[:, c, kofs:kofs + TB], BK[bi]),
                  reads=[bn(bi)], writes=["kT"])
        for hf in range(2):
            wi_ = stream_w2(hf)
            for t in range(NT):
                gt = b * NT + t
                bi = mm_next()
                for kt in range(8):
                    P.pe(lambda e, hT=hT, bi=bi, wi_=wi_, kt=kt, t=t: e.matmul(BK[bi][:, 0:256], lhsT=hT[:, kt, t * 128:(t + 1) * 128],
                                                                       rhs=W2[wi_][:, kt, :], start=(kt == 0), stop=(kt == 7)),
                         reads=["w2_%d" % wi_, HT], writes=[bn(bi)])
                P.act(lambda e, bi=bi, gt=gt, hf=hf: e.activation(
                    out=Vr[:, gt % 8, 4 * hf:4 * hf + 4, 0:64], in_=BK[bi][:, 0:256].rearrange("p (h d) -> p h d", d=64),
                    func=AF.Copy), reads=[bn(bi)], writes=["Vr"])
        for hf in range(2):
            wi_ = stream_w2(2 + hf)
            for t in range(NT):
                bi = mm_next()
                for kt in range(8):
                    P.pe(lambda e, hT=hT, bi=bi, wi_=wi_, kt=kt, t=t: e.matmul(BK[bi][:, 0:256], lhsT=hT[:, kt, t * 128:(t + 1) * 128],
                                                                       rhs=W2[wi_][:, kt, :], start=(kt == 0), stop=(kt == 7)),
                         reads=["w2_%d" % wi_, HT], writes=[bn(bi)])
                P.act(lambda e, bi=bi, t=t, hf=hf: e.activation(out=zas[:, t, 256 * hf:256 * hf + 256], in_=BK[bi][:, 0:256],
                                                                 func=AF.Silu), reads=[bn(bi)], writes=["zas"])
        for c in range(4):
            bi = proj_fm(16 + c)
            P.dve(lambda e, bi=bi, c=c: e.tensor_copy(uT[:, c, :].rearrange("p (j c) -> p j c", c=NCB),
                                                      BK[bi].rearrange("p (c j) -> p j c", j=T)),
                  reads=[bn(bi)], writes=["uT%d" % c])
        for c in range(4):
            bi = proj_fm(20 + c)
            P.act(lambda e, bi=bi, c=c: e.activation(out=zss[:, c, :].rearrange("p (j c) -> p j c", c=NCB),
                                                     in_=BK[bi].rearrange("p (c j) -> p j c", j=T), func=AF.Silu),
                  reads=[bn(bi)], writes=["zss%d" % c])
        chk(2)
        def attn_scores(t, gt, dls, h, part):
            hc, po = h // 2, 64 * (h % 2)
            for d in dls:
                if (d == 4) != (part == "d4"):
                    continue
                ks = ((gt - d) % 8) * 128
                sbk, ssl = (h % 3, d) if d < 4 else (3, h % 4)
                P.pe(lambda e, sbk=sbk, ssl=ssl, po=po, hc=hc, ks=ks, t=t: e.matmul(
                    SCB[sbk][:, ssl, :], lhsT=kT[po:po + 64, hc, ks:ks + 128], rhs=qT[po:po + 64, hc, t * 128:(t + 1) * 128],
                    start=True, stop=True), reads=["kT", "qT"], writes=["sc%d" % sbk])

        def attn_exp(t, gt, dls, h):
            pset = h % 3
            pb = PB[pset]
            pres = "PB%d" % pset
            bk = SCB[h % 3]
            te = tmpE[h % 2]
            tres = "tmpE%d" % (h % 2)
            na = len([d for d in dls if d < 2])
            nb_ = len([d for d in dls if 2 <= d < 4])
            P.act(lambda e, bk=bk, na=na, te=te: e.activation(out=te[:, 0:na, :], in_=bk[:, 0:na, :], func=AF.Exp, scale=ATT_SCALE),
                  reads=["sc%d" % (h % 3)], writes=[tres])
            P.pool(lambda e, pb=pb, na=na, h=h, te=te: e.tensor_tensor(out=pb[:, 0:na, :], in0=te[:, 0:na, :], in1=EB[:, h, 0:na, :], op=ALU.mult),
                   reads=[tres, "EB"], writes=[pres + "a"])
            if nb_:
                P.act(lambda e, pb=pb, bk=bk, nb_=nb_, h=h: e.activation(out=pb[:, 2:2 + nb_, :], in_=bk[:, 2:2 + nb_, :], func=AF.Exp,
                                                                      scale=ATT_SCALE, bias=cb[:, h:h + 1]),
                      reads=["sc%d" % (h % 3), "cb"], writes=[pres + "b"])
            if 4 in dls:
                P.act(lambda e, pb=pb, h=h: e.activation(out=pb[:, 4, :], in_=SCB[3][:, h % 4, :], func=AF.Exp, scale=ATT_SCALE,
                                                       bias=cb[:, h:h + 1]), reads=["sc3", "cb"], writes=[pres + "c"])
                P.pool(lambda e, pb=pb: e.memset(pb[0:64, 4, 64:128], 0.0), reads=[pres + "c"], writes=[pres + "c"])

        def attn_pv(t, gt, dls, h, part):
            pb = PB[h % 3]
            pvreg = PV[h // 4][:, h % 4, 0:65]
            for n_, d in enumerate(dls):
                if (d == 4) != (part == "hi"):
                    continue
                kt_ = gt - d
                P.pe(lambda e, pvreg=pvreg, pb=pb, d=d, kt_=kt_, h=h, n_=n_, nd=len(dls): e.matmul(
                    pvreg, lhsT=pb[:, d, :], rhs=Vr[:, kt_ % 8, h, :], start=(n_ == 0), stop=(n_ == nd - 1)),
                    reads=["PB%d%s" % (h % 3, "abc"[min(d, 4) // 2]), "Vr", "Vr_ones"], writes=["pv%d" % (h // 4)])

        def attn_prologue(t):
            gt = b * NT + t
            dls = [d for d in range(5) if gt - d >= 0]
            attn_scores(t, gt, dls, 0, "main")
            attn_scores(t, gt, dls, 0, "d4")
            attn_scores(t, gt, dls, 1, "main")
            attn_scores(t, gt, dls, 2, "main")

        def attn_heads(t, filler=None, act_hook=None):
            gt = b * NT + t
            dls = [d for d in range(5) if gt - d >= 0]
            for h in range(8):
                attn_exp(t, gt, dls, h)
                if act_hook is not None:
                    act_hook(h)
                attn_pv(t, gt, dls, h, "lo")
                if h + 1 < 8:
                    attn_scores(t, gt, dls, h + 1, "d4")
                attn_pv(t, gt, dls, h, "hi")
                if filler is not None:
                    filler(h)
                if h + 3 < 8:
                    attn_scores(t, gt, dls, h + 3, "main")

        def attn_tail(t):
            for hb in range(2):
                P.dve(lambda e, hb=hb: e.reciprocal(rec[:, 4 * hb:4 * hb + 4], PV[hb][:, :, 64]), reads=["pv%d" % hb],
                      writes=["rec%d" % hb])
            for h in range(8):
                P.dve(lambda e, h=h, t=t: e.scalar_tensor_tensor(
                    out=yat[:, 64 * h:64 * h + 64], in0=PV[h // 4][:, h % 4, 0:64], scalar=rec[:, h:h + 1],
                    in1=zas[:, t, 64 * h:64 * h + 64], op0=ALU.mult, op1=ALU.mult),
                    reads=["pv%d" % (h // 4), "rec%d" % (h // 4), "zas"], writes=["yat"])

        tr_bank = {}

        def attn_tr_pe(t):
            bi = mm_next()
            ring["held"].add(bi)
            tr_bank[t] = bi
            tpv = BK[bi].bitcast(BF16)[:, 0:512].rearrange("p (k c) -> p k c", c=128)
            for c in range(4):
                P.pe(lambda e, tpv=tpv, c=c: e.transpose(tpv[:, c, :], yat[:, c * 128:(c + 1) * 128], identb[:]),
                     reads=["yat", "identb"], writes=[bn(bi)])

        def attn_tr_evac(t):
            bi = tr_bank[t]
            ring["held"].discard(bi)
            tpv = BK[bi].bitcast(BF16)[:, 0:512].rearrange("p (k c) -> p k c", c=128)
            P.act(lambda e, tpv=tpv, t=t: e.activation(out=yaT[:, :, t * 128:(t + 1) * 128], in_=tpv, func=AF.Copy),
                  reads=[bn(bi)], writes=["yaT"])

        UT = ["uT%d" % c for c in range(4)]
        P.dve(lambda e: e.tensor_copy(Xh[:, :, :, 0], Xh[:, :, :, NCB]), reads=["Xh"], writes=["Xh"])
        ssm_bank = {}

        def ssm_a(q, part):
            if part == 0:
                ssm_bank[q] = mm_next()
                ring["held"].add(ssm_bank[q])
            bi = ssm_bank[q]
            Sv = BK[bi].rearrange("p (a c) -> p a c", c=NCB)
            for reim in (part // 4,):
                for kq in (part % 4,):
                    for s in range(T):
                        P.pe(lambda e, Sv=Sv, q=q, kq=kq, s=s, reim=reim: e.matmul(
                            Sv[:, 4 * reim + kq, :], lhsT=Wa[32 * q:32 * q + 32, kq, s, reim, :],
                            rhs=uT[32 * q:32 * q + 32, kq, s * NCB:(s + 1) * NCB],
                            start=(s == 0), stop=(s == T - 1), tile_position=(32 * q, 0)),
                            reads=["Wa", "uT%d" % kq], writes=[bn(bi)])

        def ssm_rec(q):
            bi = ssm_bank[q]
            ring["held"].discard(bi)
            Sv = BK[bi].rearrange("p (a c) -> p a c", c=NCB)
            for reim in range(2):
                P.dve(lambda e, Sv=Sv, reim=reim: e.tensor_copy(Sh[reim][:], Sv[:, 4 * reim:4 * reim + 4, :]),
                      reads=[bn(bi)], writes=["Sh%d" % reim])
            csv, snv = CS[:, q:16:4, :], SN[:, q:16:4, :]
            tt_("dve", Gi[0][:], Sh[0][:], csv, ALU.mult, ["Sh0"], ["Gi0"])
            tt_("dve", Gt[:], Sh[1][:], snv, ALU.mult, ["Sh1"], ["Gt"])
            tt_("dve", Gi[0][:], Gi[0][:], Gt[:], ALU.add, ["Gi0", "Gt"], ["Gi0"])
            tt_("dve", Gi[1][:], Sh[1][:], csv, ALU.mult, ["Sh1"], ["Gi1"])
            tt_("dve", Gt[:], Sh[0][:], snv, ALU.mult, ["Sh0", "Gi0"], ["Gt"])
            tt_("dve", Gi[1][:], Gi[1][:], Gt[:], ALU.subtract, ["Gi1", "Gt"], ["Gi1"])
            for reim in range(2):
                for kq in range(4):
                    p_ = 4 * kq + q
                    P.dve(lambda e, reim=reim, kq=kq, p_=p_: e.tensor_tensor_scan(
                        out=Sh[reim][:, kq, :], data0=rT[:, p_:p_ + 1].to_broadcast([128, NCB]), data1=Gi[reim][:, kq, :],
                        initial=Xl[:, reim, p_:p_ + 1], op0=ALU.mult, op1=ALU.add),
                        reads=["Gi%d" % reim, "Xl", "rT", "Sh%d" % reim], writes=["Sh%d" % reim])
            tt_("dve", Gi[0][:], Sh[0][:], csv, ALU.mult, ["Sh0"], ["Gi0"])
            tt_("dve", Gt[:], Sh[1][:], snv, ALU.mult, ["Sh1"], ["Gt"])
            tt_("dve", Gi[0][:], Gi[0][:], Gt[:], ALU.subtract, ["Gi0", "Gt"], ["Gi0"])
            tt_("dve", Gi[1][:], Sh[1][:], csv, ALU.mult, ["Sh1"], ["Gi1"])
            tt_("dve", Gt[:], Sh[0][:], snv, ALU.mult, ["Sh0", "Gi0"], ["Gt"])
            tt_("dve", Gi[1][:], Gi[1][:], Gt[:], ALU.add, ["Gi1", "Gt"], ["Gi1"])
            for reim in range(2):
                P.dve(lambda e, reim=reim, q=q: e.tensor_copy(Xh[:, reim, q:16:4, 1:NCB + 1], Gi[reim][:]),
                      reads=["Gi%d" % reim], writes=["Xh"])
                P.dve(lambda e, reim=reim, q=q: e.tensor_copy(Xl[:, reim, q:16:4], Gi[reim][:, :, NCB - 1]),
                      reads=["Gi%d" % reim], writes=["Xl"])
        nxt = b + 1 if b + 1 < nblocks else None
        if nxt is not None:
            b1_load(nxt, 0)
        def make_filler(t):
            def filler(h):
                if t + 1 < NT:
                    ssm_a(t + 1, h)
                if nxt is not None and h == 5:
                    b1_sq(nxt, t)
                if h == 3 and t >= 1:
                    attn_tr_pe(t - 1)
                if nxt is not None and h == 6 and t >= 1:
                    b1_pe(nxt, t - 1)
            return filler

        def make_hook(t):
            def hook(h):
                if h == 4 and t >= 1:
                    attn_tr_evac(t - 1)
            return hook

        ring["n"] = 2
        for part in range(8):
            ssm_a(0, part)
        ssm_rec(0)
        attn_prologue(0)
        for t in range(NT):
            attn_heads(t, filler=make_filler(t), act_hook=make_hook(t))
            if t + 1 < NT:
                attn_prologue(t + 1)
            if nxt is not None:
                b1_scale(nxt, t)
            attn_tail(t)
            if t + 1 < NT:
                ssm_rec(t + 1)
            if nxt is not None:
                if t + 1 < NT:
                    b1_load(nxt, t + 1)
        if nxt is not None:
            b1_pe(nxt, NT - 1)
        ring["n"] = 8
        chk(3)
        for kq in range(4):
            bi = mm_next()
            Y3 = BK[bi].rearrange("p (c j) -> p c j", j=T)
            U3 = uT[:, kq, :].rearrange("p (c j) -> p c j", j=T)
            for tau in range(T):
                P.pe(lambda e, Y3=Y3, U3=U3, tau=tau, kq=kq, bi=bi: e.matmul(
                    BK[bi][:, tau * NCB:TB], lhsT=Kin[:, kq, tau, :],
                    rhs=uT[:, kq, 0:(T - tau) * NCB], start=(tau == 0), stop=False),
                    reads=["Kin", "uT%d" % kq], writes=[bn(bi)])
            for q in range(4):
                p_ = 4 * kq + q
                for j in range(T):
                    for reim in range(2):
                        last = (j == T - 1 and reim == 1)
                        P.pe(lambda e, bi=bi, q=q, j=j, reim=reim, p_=p_, last=last: e.matmul(
                            BK[bi][32 * q:32 * q + 32, j * NCB:(j + 1) * NCB], lhsT=Wi[:, p_, j, reim, :], rhs=Xh[:, reim, p_, 0:NCB],
                            start=False, stop=last, tile_position=(0, 32 * q)),
                            reads=["Wi", "Xh"], writes=[bn(bi)])
            Yp = BK[bi]
            P.act(lambda e, Yp=Yp: e.activation(out=E0[:], in_=Yp, func=AF.Square), reads=[bn(bi)], writes=["E0"])
            P.dve(lambda e: e.tensor_scalar(out=E0[:], in0=E0[:], scalar1=0.044715, scalar2=1.0, op0=ALU.mult, op1=ALU.add),
                  reads=["E0"], writes=["E0"])
            tt_("dve", E0[:], E0[:], Yp, ALU.mult, ["E0", bn(bi)], ["E0"])
            P.act(lambda e: e.activation(out=E1[:], in_=E0[:], func=AF.Sigmoid, scale=1.5957691216057308), reads=["E0"], writes=["E1"])
            tt_("dve", uT[:, kq, :], E1[:], Yp, ALU.mult, ["E1", bn(bi)], ["uT%d" % kq])
        attn_tr_pe(NT - 1)
        attn_tr_evac(NT - 1)
        chk(4)
        for mt in range(4):
            bb_ = mm_next()
            for c in range(4):
                P.pe(lambda e, bb_=bb_, c=c, mt=mt: e.matmul(BK[bb_], lhsT=Wglu[:, c, 512 + mt * 128:512 + (mt + 1) * 128],
                                                            rhs=uT[:, c, :], start=(c == 0), stop=(c == 3)),
                     reads=["Wglu"] + UT, writes=[bn(bb_)])
            P.act(lambda e, bb_=bb_, mt=mt: e.activation(out=E1[:], in_=BK[bb_], func=AF.Sigmoid, bias=bgT[:, 4 + mt:5 + mt]),
                  reads=[bn(bb_), "bgT"], writes=["E1"])
            ba = mm_next()
            for c in range(4):
                P.pe(lambda e, ba=ba, c=c, mt=mt: e.matmul(BK[ba], lhsT=Wglu[:, c, mt * 128:(mt + 1) * 128], rhs=uT[:, c, :],
                                                          start=(c == 0), stop=(c == 3)), reads=["Wglu"] + UT, writes=[bn(ba)])
            P.dve(lambda e, ba=ba, mt=mt: e.scalar_tensor_tensor(out=E0[:], in0=BK[ba], scalar=bgT[:, mt:mt + 1], in1=E1[:],
                                                                 op0=ALU.add, op1=ALU.mult), reads=[bn(ba), "E1", "bgT"], writes=["E0"])
            tt_("pool", zss[:, mt, :], E0[:], zss[:, mt, :], ALU.mult, ["E0", "zss%d" % mt], ["zss%d" % mt])
        ZS = ["zss%d" % c for c in range(4)]
        chk(5)
        for dt_ in range(8):
            bga = proj_fm(24 + dt_)
            P.act(lambda e, bga=bga, dt_=dt_: e.activation(out=SG[0][:], in_=BK[bga], func=AF.Sigmoid, bias=gbT[:, dt_:dt_ + 1]),
                  reads=[bn(bga), "gbT"], writes=["SG0"])
            bgs = proj_fm(32 + dt_)
            P.act(lambda e, bgs=bgs, dt_=dt_: e.activation(out=SG[1][:], in_=BK[bgs], func=AF.Sigmoid, bias=gbT[:, 8 + dt_:9 + dt_]),
                  reads=[bn(bgs), "gbT"], writes=["SG1"])
            bya = mm_next()
            for c in range(4):
                P.pe(lambda e, bya=bya, c=c, dt_=dt_: e.matmul(BK[bya], lhsT=Wao[:, c, dt_ * 128:(dt_ + 1) * 128], rhs=yaT[:, c, :],
                                                              start=(c == 0), stop=(c == 3)), reads=["Wao", "yaT"], writes=[bn(bya)])
            tt_("dve", E0[:], BK[bya], SG[0][:], ALU.mult, [bn(bya), "SG0"], ["E0"])
            bys = mm_next()
            for c in range(4):
                P.pe(lambda e, bys=bys, c=c, dt_=dt_: e.matmul(BK[bys], lhsT=Wso[:, c, dt_ * 128:(dt_ + 1) * 128], rhs=zss[:, c, :],
                                                              start=(c == 0), stop=(c == 3)), reads=["Wso"] + ZS, writes=[bn(bys)])
            tt_("dve", E1[:, :].rearrange("p (c j) -> p c j", j=T), BK[bys].rearrange("p (j c) -> p c j", c=NCB),
                SG[1][:, :].rearrange("p (c j) -> p c j", j=T), ALU.mult, [bn(bys), "SG1"], ["E1"])
            tt_("pool", mrg[:, dt_, :], E0[:], E1[:], ALU.add, ["E0", "E1"], ["mrg"])
        chk(6)
        for t in range(NT):
            gt = b * NT + t
            xi = load_x(gt)
            for hf in range(2):
                bi = mm_next()
                for kt in range(8):
                    P.pe(lambda e, bi=bi, kt=kt, t=t, hf=hf: e.matmul(BK[bi], lhsT=mrg[:, kt, t * 128:(t + 1) * 128],
                                                                     rhs=Wout[:, kt, hf * 512:(hf + 1) * 512], start=(kt == 0), stop=(kt == 7)),
                         reads=["mrg", "Wout"], writes=[bn(bi)])
                tt_("dve", xbuf[xi][:, hf * 512:(hf + 1) * 512], BK[bi], xbuf[xi][:, hf * 512:(hf + 1) * 512], ALU.add,
                    [bn(bi), "xbuf%d" % xi], ["xbuf%d" % xi])
            P.act(lambda e, xi=xi: e.activation(out=xnb[0], in_=xbuf[xi][:], func=AF.Square, accum_out=ssq[:, 4:5]),
                  reads=["xbuf%d" % xi], writes=["E0", "ssq4"])
            P.pool(lambda e: e.tensor_scalar(out=rstd[:, 4:5], in0=ssq[:, 4:5], scalar1=1.0 / D, scalar2=1e-6, op0=ALU.mult, op1=ALU.add),
                   reads=["ssq4"], writes=["rstd4"])
            P.pool(lambda e: e.tensor_tensor(out=rstd[:, 4:5], in0=rstd[:, 4:5], in1=mhalf[:], op=ALU.pow),
                   reads=["rstd4", "mhalf"], writes=["rstd4"])
            P.dve(lambda e, xi=xi: e.scalar_tensor_tensor(out=xbuf[xi][:], in0=xbuf[xi][:], scalar=rstd[:, 4:5], in1=fgb[:],
                                                          op0=ALU.mult, op1=ALU.mult), reads=["xbuf%d" % xi, "rstd4", "fgb"],
                  writes=["xbuf%d" % xi])
            P.dma(lambda e, xi=xi, gt=gt: e.dma_start(out=out_d[gt * 128:(gt + 1) * 128, :], in_=xbuf[xi][:]),
                  reads=["xbuf%d" % xi], writes=["out"], semkey="xbo%d" % xi, eng="act")
    except _Stop:
        pass
    if maxops is not None:
        P.ops = P.ops[:maxops]
        P.barriers = [x for x in P.barriers if x <= maxops]
    stats = P.build()
    NCg.__exit__(None, None, None)
    es.close()
    return nc, stats


def host_layouts(sh):
    f = np.float32
    vecs = np.zeros((128, 92), f)
    vecs[:, 0:8] = sh["norm_gain"].reshape(8, 128).T
    vecs[:, 8:24] = sh["gate_bias"].reshape(16, 128).T
    vecs[:, 24:32] = sh["b_glu"].reshape(8, 128).T
    vecs[:, 32:36] = sh["ssm_d"].reshape(4, 128).T
    vecs[:, 36:52] = sh["ssm_a_re"].reshape(16, 2, 64).transpose(1, 2, 0).reshape(128, 16)
    vecs[:, 52:68] = sh["ssm_a_im"].reshape(16, 2, 64).transpose(1, 2, 0).reshape(128, 16)
    vecs[:, 68:84] = np.repeat(sh["ssm_log_dt"].reshape(16, 2).T[:, None, :], 64, axis=1).reshape(128, 16)
    vecs[:, 84:92] = np.repeat(sh["rel_bias"][:, 191][None, :], 128, axis=0)
    bT = np.zeros((128, 2, 16, 32), f)
    cT = np.zeros((128, 2, 16, 32), f)
    for ri, (bk, ck) in enumerate((("ssm_b_re", "ssm_c_re"), ("ssm_b_im", "ssm_c_im"))):
        bb = sh[bk].reshape(16, 2, 64, 16)
        cc = sh[ck].reshape(16, 2, 16, 64)
        for g in range(2):
            bT[64 * g:64 * g + 64, ri, :, 16 * g:16 * g + 16] = bb[:, g].transpose(1, 0, 2)
            cT[64 * g:64 * g + 64, ri, :, 16 * g:16 * g + 16] = cc[:, g].transpose(2, 0, 1)
    return {"p_vecs": vecs, "p_bT": bT, "p_cT": cT}


def kernel(**inputs):
    n = 8
    nc, stats = build_program()
    ident = np.eye(128, dtype=np.float32)
    idx = np.arange(128) // 16
    maskbd = (idx[:, None] == idx[None, :]).astype(np.float32)
    x = np.asarray(inputs["x"], dtype=np.float32)
    shared = {}
    for k, v in inputs.items():
        if k == "x":
            continue
        a = np.asarray(v, dtype=np.float32)
        if k != "final_gain":
            a = a[0]
        shared[k] = np.ascontiguousarray(a)
    shared["c_ident"] = ident
    shared["c_maskbd"] = maskbd
    rb = shared["rel_bias"]
    shared["rel_bias_x"] = np.ascontiguousarray(np.concatenate([rb, np.repeat(rb[:, 191:192], 192, axis=1)], axis=1))
    shared["c_flip"] = np.ascontiguousarray(ident[::-1])
    shared.update(host_layouts(shared))
    in_maps = []
    for c in range(n):
        m = dict(shared)
        m["x"] = np.ascontiguousarray(x[c])
        in_maps.append(m)
    res = run_bass_kernel_spmd(nc, in_maps, core_ids=list(range(n)))
    out = np.stack([np.asarray(r["out"], dtype=np.float32) for r in res.results], axis=0)
    return out
```

```python
import contextlib
import math
import numpy as np
import concourse.bass as bass
import concourse.mybir as mybir
from concourse.bass_utils import run_bass_kernel_spmd

F32 = mybir.dt.float32
BF16 = mybir.dt.bfloat16
I32 = mybir.dt.int32
AF = mybir.ActivationFunctionType
ALU = mybir.AluOpType

L = 4096
D = 1024
TB = 512
NB = L // TB
NT = TB // 128
T = 8
NCB = TB // T
ATT_SCALE = 1.0 / 8.0
ENGS = ("pe", "act", "dve", "pool", "sp")


class Prog:
    def __init__(self, nc):
        self.nc = nc
        self.ops = []
        self.barriers = []
        self.sucnt = {}

    def op(self, eng, fn, reads=(), writes=(), dma=False, semkey=None, nobar=False):
        self.ops.append(dict(eng=eng, fn=fn, reads=tuple(reads), writes=tuple(writes), dma=dma, semkey=semkey, nobar=nobar))
        return len(self.ops) - 1

    def pe(self, fn, reads=(), writes=()):
        return self.op("pe", fn, reads, writes)

    def act(self, fn, reads=(), writes=()):
        return self.op("act", fn, reads, writes)

    def dve(self, fn, reads=(), writes=()):
        return self.op("dve", fn, reads, writes)

    def pool(self, fn, reads=(), writes=()):
        return self.op("pool", fn, reads, writes)

    def dma(self, fn, reads=(), writes=(), semkey=None, eng="sp", nobar=False):
        if not (semkey.startswith("xb") or semkey.startswith("wt") or semkey.startswith("w2_")):
            c = self.sucnt.get(eng, 0)
            self.sucnt[eng] = c + 1
            semkey = "%s_su%d" % (eng, c % 4)
        return self.op(eng, fn, reads, writes, dma=True, semkey=semkey, nobar=nobar)

    def barrier(self):
        self.barriers.append(len(self.ops))

    def build(self):
        nc = self.nc
        ops = self.ops
        n = len(ops)
        last_writer = {}
        readers = {}
        deps = [set() for _ in range(n)]
        for i, o in enumerate(ops):
            for r in o["reads"]:
                if r in last_writer:
                    deps[i].add(last_writer[r])
            for w in o["writes"]:
                if w in last_writer:
                    deps[i].add(last_writer[w])
                for rd in readers.get(w, ()):
                    deps[i].add(rd)
            for r in o["reads"]:
                readers.setdefault(r, []).append(i)
            for w in o["writes"]:
                last_writer[w] = i
                readers[w] = []
            deps[i].discard(i)
        last_dma = {}
        for i, o in enumerate(ops):
            if o["dma"]:
                if o["semkey"] in last_dma:
                    deps[i].add(last_dma[o["semkey"]])
                last_dma[o["semkey"]] = i
        for bidx in self.barriers:
            pre_last = {}
            pre_dmas = []
            for i in range(bidx):
                o = ops[i]
                if o["dma"]:
                    if not o["nobar"]:
                        pre_dmas.append(i)
                else:
                    pre_last[o["eng"]] = i
            seen = set()
            for i in range(bidx, n):
                e = ops[i]["eng"]
                if e in seen:
                    continue
                seen.add(e)
                for v in pre_last.values():
                    deps[i].add(v)
                for v in pre_dmas:
                    deps[i].add(v)
                if len(seen) == len(ENGS):
                    break
        for i, o in enumerate(ops):
            rm = set()
            for d in deps[i]:
                p = ops[d]
                if p["dma"]:
                    continue
                if p["eng"] == o["eng"] and o["eng"] == "pe" and not o["dma"]:
                    rm.add(d)
            deps[i] -= rm
        needed = set()
        for i in range(n):
            needed |= deps[i]
        es = contextlib.ExitStack()
        CAP = 16000
        sems = {}
        gen = {}
        cnt = {}
        tick = {}

        def next_tick(base, inc):
            g = gen.get(base, 0)
            c = cnt.get((base, g), 0)
            if c + inc > CAP:
                g += 1
                gen[base] = g
                c = 0
            c += inc
            cnt[(base, g)] = c
            key = (base, g)
            if key not in sems:
                sems[key] = es.enter_context(nc.semaphore("s%d" % len(sems)))
            return key, c

        for i, o in enumerate(ops):
            if o["dma"]:
                tick[i] = next_tick(("dma", o["semkey"]), 16)
            elif i in needed:
                tick[i] = next_tick(("eng", o["eng"]), 1)
        per_eng = {e: [] for e in ENGS}
        for i, o in enumerate(ops):
            per_eng[o["eng"]].append(i)
        engobj = {"pe": "tensor", "act": "scalar", "dve": "vector", "pool": "gpsimd", "sp": "sync"}
        final_ticks = {}
        for i, o in enumerate(ops):
            if o["dma"]:
                final_ticks[tick[i][0]] = tick[i][1]
        with nc.Block() as block:
            for e in ENGS:
                idxs = per_eng[e]

                def body(eng, idxs=idxs, e=e):
                    waited = {}
                    for i in idxs:
                        o = ops[i]
                        for d in sorted(deps[i]):
                            k, v = tick[d]
                            if waited.get(k, 0) >= v:
                                continue
                            eng.wait_ge(sems[k], v)
                            waited[k] = v
                        ins = o["fn"](eng)
                        if i in tick:
                            k, v = tick[i]
                            ins.then_inc(sems[k], 16 if o["dma"] else 1)
                    if e == "sp":
                        for k, v in final_ticks.items():
                            if waited.get(k, 0) < v:
                                eng.wait_ge(sems[k], v)

                getattr(block, engobj[e])(body)
        es.close()
        return {e: len(per_eng[e]) for e in ENGS}


class _Stop(Exception):
    pass


def build_program(nblocks=NB, stage=99, maxops=None, marks=None):
    nc = bass.Bass("TRN2", target_bir_lowering=False)
    P = Prog(nc)

    def din(name, shape, dt=F32):
        return nc.dram_tensor(name, list(shape), dt, kind="ExternalInput")

    x_t = din("x", [L, D])
    ng_t = din("norm_gain", [D])
    win_t = din("w_in", [D, 5120])
    rb_t = din("rel_bias", [8, 192])
    are_t = din("ssm_a_re", [32, 64])
    aim_t = din("ssm_a_im", [32, 64])
    ldt_t = din("ssm_log_dt", [32])
    bre_t = din("ssm_b_re", [32, 64, 16])
    bim_t = din("ssm_b_im", [32, 64, 16])
    cre_t = din("ssm_c_re", [32, 16, 64])
    cim_t = din("ssm_c_im", [32, 16, 64])
    sd_t = din("ssm_d", [512])
    wglu_t = din("w_glu", [512, 1024])
    bglu_t = din("b_glu", [1024])
    wao_t = din("w_attn_out", [512, 1024])
    wso_t = din("w_ssm_out", [512, 1024])
    gb_t = din("gate_bias", [2048])
    wout_t = din("w_out", [D, D])
    fg_t = din("final_gain", [D])
    ident_t = din("c_ident", [128, 128])
    mask_t = din("c_maskbd", [128, 128])
    flip_t = din("c_flip", [128, 128])
    vecs_t = din("p_vecs", [128, 92])
    bT_t = din("p_bT", [128, 2, 16, 32])
    cT_t = din("p_cT", [128, 2, 16, 32])
    out_t = nc.dram_tensor("out", [L, D], F32, kind="ExternalOutput")
    wsc_t = nc.dram_tensor("wsc", [40, 128, 8, 128], BF16, kind="Internal")
    wsc2_t = nc.dram_tensor("wsc2", [4, 128, 8, 256], BF16, kind="Internal")
    rbx_t = din("rel_bias_x", [8, 384])
    x_d, out_d, win_d = x_t.ap(), out_t.ap(), win_t.ap()
    wsc_d, wsc2_d, rbx_d = wsc_t.ap(), wsc2_t.ap(), rbx_t.ap()

    es = contextlib.ExitStack()

    def sb(name, shape, dt=F32, stack=None):
        return (stack or es).enter_context(nc.sbuf_tensor(name, list(shape), dt))

    def psum(name, shape, dt=F32):
        return es.enter_context(nc.psum_tensor(name, list(shape), dt))

    NCg = nc.allow_non_contiguous_dma(reason="small setup loads")
    NCg.__enter__()

    Wglu = sb("Wglu", [128, 4, 1024], BF16)
    Wao = sb("Wao", [128, 4, 1024], BF16)
    Wso = sb("Wso", [128, 4, 1024], BF16)
    Wout = sb("Wout", [128, 8, 1024], BF16)
    Wa = sb("Wa", [128, 4, T, 2, 128], BF16)
    Wi = sb("Wi", [128, 16, T, 2, 32], BF16)
    Kin = sb("Kin", [128, 4, T, 128], BF16)
    CS = sb("CS", [128, 16, NCB])
    SN = sb("SN", [128, 16, NCB])
    rT = sb("rT", [128, 16])
    Xl = sb("Xl", [128, 2, 16])
    Xh = sb("Xh", [128, 2, 16, NCB + 1], BF16)
    identb = sb("identb", [128, 128], BF16)
    vecs = sb("vecs", [128, 92])
    g1T, gbT, bgT, cb = vecs[:, 0:8], vecs[:, 8:24], vecs[:, 24:32], vecs[:, 84:92]
    fgb = sb("fgb", [128, 1024])
    EB = sb("EB", [128, 8, 2, 128], BF16)
    mhalf = sb("mhalf", [128, 1])
    PB = [sb("PB%d" % s, [128, 5, 128], BF16) for s in range(3)]

    MM = [psum("mm%d" % i, [128, 512]) for i in range(2)]
    SCB = [psum("sc%d" % i, [128, 4, 128]) for i in range(4)]
    PV = [psum("pv%d" % i, [128, 4, 128]) for i in range(2)]
    BK = [MM[0][:, :], MM[1][:, :]] + [SCB[i][:, :, :].rearrange("p a b -> p (a b)") for i in range(4)] + \
         [PV[i][:, :, :].rearrange("p a b -> p (a b)") for i in range(2)]
    BKN = ["mm0", "mm1", "sc0", "sc1", "sc2", "sc3", "pv0", "pv1"]
    ring = {"n": 8, "c": 0, "held": set()}

    def mm_next():
        for _ in range(16):
            i = ring["c"] % ring["n"]
            ring["c"] += 1
            if i not in ring["held"]:
                return i
        raise RuntimeError("no free PSUM bank in ring")

    def bn(i):
        return BKN[i]

    xbuf = [sb("xbuf%d" % i, [128, 1024]) for i in range(3)]
    hTs = [sb("hT%d" % i, [128, 8, TB], BF16) for i in range(2)]
    E0 = sb("E0", [128, 512]); E1 = sb("E1", [128, 512])
    ssq = sb("ssq", [128, 8]); rstd = sb("rstd", [128, 8]); rec = sb("rec", [128, 8])
    xnb = [E0[:, :].bitcast(BF16), E1[:, :].bitcast(BF16)]
    xnres = ["E0", "E1"]
    b1x = {}
    ss = contextlib.ExitStack()
    identf = sb("identf", [128, 128], F32, ss)
    maskbd = sb("maskbd", [128, 128], F32, ss)
    P.dma(lambda e: e.dma_start(out=identf[:], in_=ident_t.ap()), writes=["identf"], semkey="c0")
    P.dma(lambda e: e.dma_start(out=maskbd[:], in_=mask_t.ap()), writes=["maskbd"], semkey="c1")
    P.dve(lambda e: e.tensor_copy(identb[:], identf[:]), reads=["identf"], writes=["identb"])
    P.dve(lambda e: e.memset(mhalf[:], -0.5), writes=["mhalf"])
    P.dma(lambda e: e.dma_start(out=vecs[:], in_=vecs_t.ap()),
          writes=["g1T", "gbT", "bgT", "cb", "lre", "lim", "ldt0", "ldt1", "dT"], semkey="c2")
    P.dma(lambda e: e.dma_start(out=fgb[:], in_=fg_t.ap().rearrange("(o d) -> o d", o=1).partition_broadcast(128)),
          writes=["fgb"], semkey="c5")
    xc = [0]

    def load_x(gt):
        i = xc[0] % 3
        xc[0] += 1
        P.dma(lambda e, i=i, gt=gt: e.dma_start(out=xbuf[i][:], in_=x_d[gt * 128:(gt + 1) * 128, :]),
              writes=["xbuf%d" % i], semkey="xb%d" % i)
        return i

    for t in range(3):
        b1x[(0, t)] = load_x(t)
    if marks is not None:
        marks['a_consts'] = len(P.ops)
    win_v = win_d.rearrange("(kt p) c -> p kt c", p=128)
    ct_order = list(range(0, 8)) + ["v", "za"] + list(range(16, 40))

    def cast_ct(ct):
        P.dma(lambda e, ct=ct: e.dma_start(out=wsc_d[ct], in_=win_v[:, :, ct * 128:(ct + 1) * 128]),
              writes=["wsc%d" % ct], semkey="wc%d" % (ct % 4), eng="pool", nobar=True)

    def cast_w2(idx):
        c0 = 1024 + idx * 256
        P.dma(lambda e, idx=idx, c0=c0: e.dma_start(out=wsc2_d[idx], in_=win_v[:, :, c0:c0 + 256]),
              writes=["wsc2_%d" % idx], semkey="w2c%d" % (idx % 2), eng="pool", nobar=True)

    for ct in ct_order:
        if ct == "v":
            cast_w2(0), cast_w2(1)
        elif ct == "za":
            cast_w2(2), cast_w2(3)
        else:
            cast_ct(ct)
    if marks is not None:
        marks['b_wsc'] = len(P.ops)
    for nm, wt_, src, nk in (("Wglu", Wglu, wglu_t, 4), ("Wao", Wao, wao_t, 4), ("Wso", Wso, wso_t, 4), ("Wout", Wout, wout_t, 8)):
        sv = src.ap().rearrange("(kt p) c -> p kt c", p=128)
        for kt in range(nk):
            P.dma(lambda e, wt_=wt_, sv=sv, kt=kt: e.dma_start(out=wt_[:, kt, :], in_=sv[:, kt, :]),
                  writes=[nm], semkey="wr%d" % (kt % 2), eng="pool", nobar=True)

    if marks is not None:
        marks['c_resw'] = len(P.ops)
    ebraw = sb("ebraw", [128, 8, 2, 128], F32, ss)
    flipf = sb("flipf", [128, 128], F32, ss)
    P.dma(lambda e: e.dma_start(out=flipf[:], in_=flip_t.ap()), writes=["flipf"], semkey="c9")
    P.dve(lambda e: e.memset(ebraw[:], 0.0), writes=["ebraw"])
    for h in range(8):
        base = h * 384
        P.dma(lambda e, h=h, base=base: e.dma_start(out=ebraw[:, h, 1, :], in_=bass.AP(rbx_t, base + 64, [[1, 128], [1, 128]])),
              reads=["ebraw"], writes=["ebraw%d_1" % h], semkey="eb%d" % (h % 4))
        P.dma(lambda e, h=h, base=base: e.dma_start(out=ebraw[64:128, h, 0, :], in_=bass.AP(rbx_t, base, [[1, 64], [1, 128]])),
              reads=["ebraw"], writes=["ebraw%d_0a" % h], semkey="eba%d" % (h % 4))
        P.dma(lambda e, h=h, base=base: e.dma_start(out=ebraw[0:64, h, 0, 64:128], in_=bass.AP(rbx_t, base, [[1, 64], [1, 64]])),
              reads=["ebraw"], writes=["ebraw%d_0b" % h], semkey="ebb%d" % (h % 4))
    allraw = [w for o in P.ops for w in o["writes"] if w.startswith("ebraw")]
    if marks is not None:
        marks['d_bias'] = len(P.ops)
    def st(name, shape, dt=F32):
        return sb(name, shape, dt, ss)

    lre, lim, ldt, dT = vecs[:, 36:52], vecs[:, 52:68], vecs[:, 68:84], vecs[:, 32:36]
    bTx = st("bTx", [128, 2, 16, 32]); cTx = st("cTx", [128, 2, 16, 32])
    P.dma(lambda e: e.dma_start(out=bTx[:], in_=bT_t.ap()), writes=["bTr", "bTr_0", "bTr_1", "bTi", "bTi_0", "bTi_1"], semkey="s3")
    P.dma(lambda e: e.dma_start(out=cTx[:], in_=cT_t.ap()), writes=["cblkrT", "cblkiT"], semkey="s4")
    bTr, bTi, cTr, cTi = bTx[:, 0], bTx[:, 1], cTx[:, 0], cTx[:, 1]
    if marks is not None:
        marks['e_ssmloads'] = len(P.ops)
    RCT = ["cblkrT", "cblkiT"]

    def tt_(eng, out, a, b, op, reads, writes):
        getattr(P, eng)(lambda e: e.tensor_tensor(out=out, in0=a, in1=b, op=op), reads=reads, writes=writes)

    if marks is not None:
        marks['f_ctrans'] = len(P.ops)
    names = ["dtv", "th", "lr", "mag", "kq", "kf", "thr", "thc", "sn", "cs", "abr", "abi", "nr", "den", "rden",
             "fr", "fi", "t1", "t2", "t3"]
    V = {nm: st("v_" + nm, [128, 16]) for nm in names}
    ki = st("v_ki", [128, 16], I32)
    SA = ["lre", "lim", "ldt0", "ldt1"]
    P.act(lambda e: e.activation(out=V["dtv"][:], in_=ldt[:], func=AF.Exp), reads=SA, writes=["dtv"])
    tt_("dve", V["th"][:], lim[:], V["dtv"][:], ALU.mult, SA + ["dtv"], ["th"])
    tt_("dve", V["lr"][:], lre[:], V["dtv"][:], ALU.mult, SA + ["dtv"], ["lr"])
    P.act(lambda e: e.activation(out=V["mag"][:], in_=V["lr"][:], func=AF.Exp), reads=["lr"], writes=["mag"])
    P.act(lambda e: e.activation(out=rT[:], in_=V["lr"][:], func=AF.Exp, scale=float(T)), reads=["lr"], writes=["rT"])
    TWO_PI = 2.0 * math.pi

    def range_reduce(dst, src_name, shift):
        P.dve(lambda e: e.tensor_scalar(out=V["kq"][:], in0=V[src_name][:], scalar1=shift, scalar2=1.0 / TWO_PI,
                                        op0=ALU.add, op1=ALU.mult), reads=[src_name], writes=["kq"])
        P.dve(lambda e: e.tensor_copy(ki[:], V["kq"][:]), reads=["kq"], writes=["ki"])
        P.dve(lambda e: e.tensor_copy(V["kf"][:], ki[:]), reads=["ki"], writes=["kf"])
        P.dve(lambda e: e.tensor_scalar(out=V["kq"][:], in0=V[src_name][:], scalar1=shift, scalar2=None, op0=ALU.add),
              reads=[src_name, "kf"], writes=["kq"])
        P.dve(lambda e: e.scalar_tensor_tensor(out=V[dst][:], in0=V["kf"][:], scalar=-TWO_PI, in1=V["kq"][:],
                                               op0=ALU.mult, op1=ALU.add), reads=["kf", "kq"], writes=[dst])

    range_reduce("thr", "th", 0.0)
    P.act(lambda e: e.activation(out=V["sn"][:], in_=V["thr"][:], func=AF.Sin), reads=["thr"], writes=["sn"])
    range_reduce("thc", "th", math.pi / 2.0)
    P.act(lambda e: e.activation(out=V["cs"][:], in_=V["thc"][:], func=AF.Sin), reads=["thc"], writes=["cs"])
    tt_("dve", V["abr"][:], V["mag"][:], V["cs"][:], ALU.mult, ["mag", "cs"], ["abr"])
    tt_("dve", V["abi"][:], V["mag"][:], V["sn"][:], ALU.mult, ["mag", "sn"], ["abi"])
    P.dve(lambda e: e.tensor_scalar(out=V["nr"][:], in0=V["abr"][:], scalar1=-1.0, scalar2=None, op0=ALU.add),
          reads=["abr"], writes=["nr"])
    tt_("dve", V["t1"][:], lre[:], lre[:], ALU.mult, SA, ["t1"])
    tt_("dve", V["t2"][:], lim[:], lim[:], ALU.mult, SA, ["t2"])
    tt_("dve", V["den"][:], V["t1"][:], V["t2"][:], ALU.add, ["t1", "t2"], ["den"])
    P.dve(lambda e: e.reciprocal(V["rden"][:], V["den"][:]), reads=["den"], writes=["rden"])
    tt_("dve", V["t1"][:], V["nr"][:], lre[:], ALU.mult, ["nr"] + SA, ["t1"])
    tt_("dve", V["t2"][:], V["abi"][:], lim[:], ALU.mult, ["abi"] + SA, ["t2"])
    tt_("dve", V["t3"][:], V["t1"][:], V["t2"][:], ALU.add, ["t1", "t2"], ["t3"])
    tt_("dve", V["fr"][:], V["t3"][:], V["rden"][:], ALU.mult, ["t3", "rden"], ["fr"])
    tt_("dve", V["t1"][:], V["abi"][:], lre[:], ALU.mult, ["abi"] + SA, ["t1"])
    tt_("dve", V["t2"][:], V["nr"][:], lim[:], ALU.mult, ["nr"] + SA, ["t2"])
    tt_("dve", V["t3"][:], V["t1"][:], V["t2"][:], ALU.subtract, ["t1", "t2"], ["t3"])
    tt_("dve", V["fi"][:], V["t3"][:], V["rden"][:], ALU.mult, ["t3", "rden"], ["fi"])
    if marks is not None:
        marks['g_scal'] = len(P.ops)
    def b1_load(bb, t):
        b1x[(bb, t)] = load_x(bb * NT + t)

    def b1_act(bb, t):
        b1_sq(bb, t)
        b1_scale(bb, t)

    def b1_sq(bb, t):
        xi = b1x[(bb, t)]
        xw, xres = xnb[t % 2], xnres[t % 2]
        P.act(lambda e, xi=xi, t=t, xw=xw: e.activation(out=xw, in_=xbuf[xi][:], func=AF.Square, accum_out=ssq[:, t:t + 1]),
              reads=["xbuf%d" % xi], writes=[xres, "ssq%d" % t])
        P.pool(lambda e, t=t: e.tensor_scalar(out=rstd[:, t:t + 1], in0=ssq[:, t:t + 1], scalar1=1.0 / D, scalar2=1e-6,
                                              op0=ALU.mult, op1=ALU.add), reads=["ssq%d" % t], writes=["rstd%d" % t])
        P.pool(lambda e, t=t: e.tensor_tensor(out=rstd[:, t:t + 1], in0=rstd[:, t:t + 1], in1=mhalf[:], op=ALU.pow),
               reads=["rstd%d" % t, "mhalf"], writes=["rstd%d" % t])

    def b1_scale(bb, t):
        xi = b1x[(bb, t)]
        xw, xres = xnb[t % 2], xnres[t % 2]
        P.act(lambda e, xi=xi, t=t, xw=xw: e.activation(out=xw, in_=xbuf[xi][:], func=AF.Copy, scale=rstd[:, t:t + 1]),
              reads=["xbuf%d" % xi, "rstd%d" % t], writes=[xres])

    def b1_pe(bb, t):
        hTw = hTs[bb % 2]
        hres = "hT%d" % (bb % 2)
        xw, xres = xnb[t % 2], xnres[t % 2]
        bi = mm_next()
        tpv = BK[bi].bitcast(BF16).rearrange("p (k c) -> p k c", c=128)
        for kt in range(8):
            P.pe(lambda e, tpv=tpv, kt=kt, xw=xw: e.transpose(tpv[:, kt, :], xw[:, kt * 128:(kt + 1) * 128], identb[:]),
                 reads=[xres, "identb"], writes=[bn(bi)])
        P.dve(lambda e, tpv=tpv, t=t, hTw=hTw: e.tensor_tensor(out=hTw[:, :, t * 128:(t + 1) * 128], in0=tpv,
                                                              in1=g1T[:, :].unsqueeze(2).to_broadcast([128, 8, 128]), op=ALU.mult),
              reads=[bn(bi), "g1T"], writes=[hres])


    b0_banks = {}

    def b1_front0(t):
        if (0, t) not in b1x:
            b1_load(0, t)
        xi = b1x[(0, t)]
        xw, xres = xnb[t % 2], xnres[t % 2]
        P.act(lambda e, xi=xi, t=t, xw=xw: e.activation(out=xw, in_=xbuf[xi][:], func=AF.Square, accum_out=ssq[:, t:t + 1]),
              reads=["xbuf%d" % xi], writes=[xres, "ssq%d" % t])
        P.act(lambda e, t=t: e.activation(out=rstd[:, t:t + 1], in_=ssq[:, t:t + 1], func=AF.Ln, scale=1.0 / D, bias=1e-6),
              reads=["ssq%d" % t], writes=["rstd%d" % t])
        P.act(lambda e, t=t: e.activation(out=rstd[:, t:t + 1], in_=rstd[:, t:t + 1], func=AF.Exp, scale=-0.5),
              reads=["rstd%d" % t], writes=["rstd%d" % t])
        b1_scale(0, t)
        bi = mm_next()
        ring["held"].add(bi)
        b0_banks[t] = bi
        tpv = BK[bi].bitcast(BF16).rearrange("p (k c) -> p k c", c=128)
        for kt in range(8):
            P.pe(lambda e, tpv=tpv, kt=kt, xw=xw: e.transpose(tpv[:, kt, :], xw[:, kt * 128:(kt + 1) * 128], identb[:]),
                 reads=[xres, "identb"], writes=[bn(bi)])

    def b1_back0(t):
        bi = b0_banks[t]
        ring["held"].discard(bi)
        tpv = BK[bi].bitcast(BF16).rearrange("p (k c) -> p k c", c=128)
        P.dve(lambda e, tpv=tpv, t=t: e.tensor_tensor(out=hTs[0][:, :, t * 128:(t + 1) * 128], in0=tpv,
                                                      in1=g1T[:, :].unsqueeze(2).to_broadcast([128, 8, 128]), op=ALU.mult),
              reads=[bn(bi), "g1T"], writes=["hT0"])

    for t in range(NT):
        b1_front0(t)
    Pr = st("Pr", [128, T + 1, 16]); Pi = st("Pi", [128, T + 1, 16])
    P.dve(lambda e: e.memset(Pr[:, 0, :], 1.0), writes=["Pr0"])
    P.dve(lambda e: e.memset(Pi[:, 0, :], 0.0), writes=["Pi0"])
    for k in range(T):
        a, b_ = "Pr%d" % k, "Pi%d" % k
        tt_("dve", V["t1"][:], Pr[:, k, :], V["abr"][:], ALU.mult, [a, "abr"], ["t1"])
        tt_("dve", V["t2"][:], Pi[:, k, :], V["abi"][:], ALU.mult, [b_, "abi"], ["t2"])
        tt_("dve", Pr[:, k + 1, :], V["t1"][:], V["t2"][:], ALU.subtract, ["t1", "t2"], ["Pr%d" % (k + 1)])
        tt_("dve", V["t1"][:], Pr[:, k, :], V["abi"][:], ALU.mult, [a, "abi"], ["t1"])
        tt_("dve", V["t2"][:], Pi[:, k, :], V["abr"][:], ALU.mult, [b_, "abr"], ["t2"])
        tt_("dve", Pi[:, k + 1, :], V["t1"][:], V["t2"][:], ALU.add, ["t1", "t2"], ["Pi%d" % (k + 1)])
    P.dve(lambda e: e.reciprocal(V["t3"][:], rT[:]), reads=["rT"], writes=["t3"])
    tt_("dve", CS[:, :, 0], Pr[:, T, :], V["t3"][:], ALU.mult, ["Pr%d" % T, "t3"], ["CS"])
    tt_("dve", SN[:, :, 0], Pi[:, T, :], V["t3"][:], ALU.mult, ["Pi%d" % T, "t3"], ["SN"])
    tb1 = st("tb1", [128, 16, 32]); tb2 = st("tb2", [128, 16, 32])
    w = 1
    while w < NCB:
        cbr = CS[:, :, w - 1:w].to_broadcast([128, 16, w])
        sbr = SN[:, :, w - 1:w].to_broadcast([128, 16, w])
        src_c, src_s = CS[:, :, 0:w], SN[:, :, 0:w]
        t1v, t2v = tb1[:, :, 0:w], tb2[:, :, 0:w]
        tt_("dve", t1v, src_c, cbr, ALU.mult, ["CS", "SN"], ["tb1"])
        tt_("dve", t2v, src_s, sbr, ALU.mult, ["CS", "SN"], ["tb2"])
        tt_("dve", t1v, t1v, t2v, ALU.subtract, ["tb1", "tb2"], ["tb1"])
        tt_("dve", t2v, src_c, sbr, ALU.mult, ["CS", "SN", "tb1"], ["tb2"])
        tt_("dve", CS[:, :, w:2 * w], t1v, t1v, ALU.max, ["tb1", "tb2"], ["CS"])
        tt_("dve", t1v, src_s, cbr, ALU.mult, ["CS", "SN"], ["tb1"])
        tt_("dve", SN[:, :, w:2 * w], t1v, t2v, ALU.add, ["tb1", "tb2"], ["SN"])
        w *= 2
    if marks is not None:
        marks['h_tables'] = len(P.ops)
    bbr = st("bbr", [128, 16, 32]); bbi = st("bbi", [128, 16, 32])
    u1 = st("u1", [128, 16, 32]); u2 = st("u2", [128, 16, 32])
    BR = ["bTr", "bTr_0", "bTr_1", "bTi", "bTi_0", "bTi_1"]

    def bc(v_):
        return v_.unsqueeze(2).to_broadcast([128, 16, 32])

    tt_("dve", u1[:], bTr[:], bc(V["fr"][:]), ALU.mult, BR + ["fr"], ["u1"])
    tt_("dve", u2[:], bTi[:], bc(V["fi"][:]), ALU.mult, BR + ["fi"], ["u2"])
    tt_("dve", bbr[:], u1[:], u2[:], ALU.subtract, ["u1", "u2"], ["bbr"])
    tt_("dve", u1[:], bTi[:], bc(V["fr"][:]), ALU.mult, BR + ["fr"], ["u1"])
    tt_("dve", u2[:], bTr[:], bc(V["fi"][:]), ALU.mult, BR + ["fi"], ["u2"])
    tt_("dve", bbi[:], u1[:], u2[:], ALU.add, ["u1", "u2"], ["bbi"])
    BpTr = st("BpTr", [128, T, 16, 32], BF16); BpTi = st("BpTi", [128, T, 16, 32], BF16)
    for k in range(T):
        pr, pi = bc(Pr[:, k, :]), bc(Pi[:, k, :])
        tt_("dve", u1[:], bbr[:], pr, ALU.mult, ["bbr", "Pr%d" % k], ["u1"])
        tt_("dve", u2[:], bbi[:], pi, ALU.mult, ["bbi", "Pi%d" % k], ["u2"])
        tt_("dve", BpTr[:, k, :, :], u1[:], u2[:], ALU.subtract, ["u1", "u2"], ["BpTr%d" % k])
        tt_("dve", u1[:], bbr[:], pi, ALU.mult, ["bbr", "Pi%d" % k], ["u1"])
        tt_("dve", u2[:], bbi[:], pr, ALU.mult, ["bbi", "Pr%d" % k], ["u2"])
        tt_("dve", BpTi[:, k, :, :], u1[:], u2[:], ALU.add, ["u1", "u2"], ["BpTi%d" % k])
    if marks is not None:
        marks['i_bpt'] = len(P.ops)
    for kq in range(4):
        for reim, src in ((0, BpTr), (1, BpTi)):
            bi = mm_next()
            tpv = BK[bi].bitcast(BF16)[:, 0:T * 128].rearrange("p (s c) -> p s c", c=128)
            for s in range(T):
                k = T - 1 - s
                P.pe(lambda e, tpv=tpv, s=s, src=src, k=k, kq=kq: e.transpose(
                    tpv[:, s, :], src[:, k, 4 * kq:4 * kq + 4, :].rearrange("p a b -> p (a b)"), identb[:]),
                    reads=["BpT%s%d" % ("ri"[reim], k), "identb"], writes=[bn(bi)])
            P.act(lambda e, tpv=tpv, kq=kq, reim=reim: e.activation(out=Wa[:, kq, :, reim, :], in_=tpv, func=AF.Copy),
                  reads=[bn(bi)], writes=["Wa"])
    if marks is not None:
        marks['j_wa'] = len(P.ops)
    for j in range(T):
        pr, pi = bc(Pr[:, j + 1, :]), bc(Pi[:, j + 1, :])
        rd = RCT + ["Pr%d" % (j + 1), "Pi%d" % (j + 1)]
        tt_("dve", u1[:], cTr[:], pr, ALU.mult, rd, ["u1"])
        tt_("dve", u2[:], cTi[:], pi, ALU.mult, rd, ["u2"])
        tt_("dve", Wi[:, :, j, 0, :], u1[:], u2[:], ALU.subtract, ["u1", "u2"], ["Wi"])
        tt_("dve", u1[:], cTr[:], pi, ALU.mult, rd, ["u1"])
        tt_("dve", u2[:], cTi[:], pr, ALU.mult, rd, ["u2"])
        P.dve(lambda e, j=j: e.scalar_tensor_tensor(out=Wi[:, :, j, 1, :], in0=u1[:], scalar=-1.0, in1=u2[:],
                                                    op0=ALU.mult, op1=ALU.subtract), reads=["u1", "u2"], writes=["Wi"])
    if marks is not None:
        marks['k_wi'] = len(P.ops)
    cTrb = st("cTrb", [128, 16, 32], BF16); ncTib = st("ncTib", [128, 16, 32], BF16)
    P.dve(lambda e: e.tensor_copy(cTrb[:], cTr[:]), reads=RCT, writes=["cTrb"])
    P.dve(lambda e: e.tensor_scalar(out=ncTib[:], in0=cTi[:], scalar1=-1.0, scalar2=None, op0=ALU.mult),
          reads=RCT, writes=["ncTib"])
    kt1 = st("kt1", [128, 128])
    for kq in range(4):
        for tau in range(T):
            bi = mm_next()
            ov = BK[bi][:, 0:128]
            sl = slice(4 * kq, 4 * kq + 4)
            P.pe(lambda e, ov=ov, tau=tau, sl=sl: e.matmul(
                ov, lhsT=BpTr[:, tau, sl, :].rearrange("p a b -> p (a b)"),
                rhs=cTrb[:, sl, :].rearrange("p a b -> p (a b)"), start=True, stop=False),
                reads=["BpTr%d" % tau, "cTrb"], writes=[bn(bi)])
            P.pe(lambda e, ov=ov, tau=tau, sl=sl: e.matmul(
                ov, lhsT=BpTi[:, tau, sl, :].rearrange("p a b -> p (a b)"),
                rhs=ncTib[:, sl, :].rearrange("p a b -> p (a b)"), start=False, stop=True),
                reads=["BpTi%d" % tau, "ncTib"], writes=[bn(bi)])
            if tau == 0:
                tt_("dve", kt1[:], ov, maskbd[:], ALU.mult, [bn(bi), "maskbd"], ["kt1"])
                P.dve(lambda e, kq=kq: e.scalar_tensor_tensor(out=Kin[:, kq, 0, :], in0=identf[:], scalar=dT[:, kq:kq + 1],
                                                             in1=kt1[:], op0=ALU.mult, op1=ALU.add),
                      reads=["kt1", "identf", "dT"], writes=["Kin"])
            else:
                tt_("dve", Kin[:, kq, tau, :], ov, maskbd[:], ALU.mult, [bn(bi), "maskbd"], ["Kin"])
    for h0 in range(0, 8, 2):
        bi = mm_next()
        for k_ in range(4):
            hh, dd = h0 + k_ // 2, k_ % 2
            P.pe(lambda e, bi=bi, k_=k_, hh=hh, dd=dd: e.matmul(BK[bi][:, k_ * 128:(k_ + 1) * 128], lhsT=flipf[:], rhs=ebraw[:, hh, dd, :],
                                                              start=True, stop=True), reads=allraw + ["flipf"], writes=[bn(bi)])
        P.act(lambda e, bi=bi, h0=h0: e.activation(out=EB[:, h0:h0 + 2, :, :].rearrange("p a b c -> p (a b c)"), in_=BK[bi],
                                                   func=AF.Exp), reads=[bn(bi)], writes=["EB"])
    P.dve(lambda e: e.memset(EB[64:128, :, 0, 0:64], 0.0), reads=["EB"], writes=["EB"])

    P.dve(lambda e: e.memset(Xl[:], 0.0), writes=["Xl"])
    P.dve(lambda e: e.memset(Xh[:], 0.0), writes=["Xh"])
    if marks is not None:
        marks['l_kin'] = len(P.ops)
    for t in range(NT):
        b1_back0(t)
    ss.close()
    P.barrier()

    qT = sb("qT", [128, 4, TB], BF16)
    kT = sb("kT", [128, 4, 2 * TB], BF16)
    Vr = sb("Vr", [128, 8, 8, 65], BF16)
    zas = sb("zas", [128, NT, 512], BF16)
    uT = sb("uT", [128, 4, TB], BF16)
    zss = sb("zss", [128, 4, TB], BF16)
    yaT = sb("yaT", [128, 4, TB], BF16)
    yat = sb("yat", [128, 512], BF16)
    mrg = sb("mrg", [128, 8, TB], BF16)
    WT = [sb("wt%d" % i, [128, 8, 128], BF16) for i in range(4)]
    W2 = [sb("w2_%d" % i, [128, 8, 256], BF16) for i in range(2)]
    tmpE = [sb("tmpE%d" % i, [128, 2, 128]) for i in range(2)]
    Sh = [sb("Sh%d" % i, [128, 4, NCB]) for i in range(2)]
    Gi = [sb("Gi%d" % i, [128, 4, NCB]) for i in range(2)]
    Gt = sb("Gt", [128, 4, NCB])
    SG = [sb("SG%d" % i, [128, 512], BF16) for i in range(2)]
    P.dve(lambda e: e.memset(Vr[:, :, :, 64:65], 1.0), writes=["Vr_ones"])

    wtc = [0]
    w2c = [0]
    scc = [0]

    def stream_ct(ct):
        i = wtc[0] % 4
        wtc[0] += 1
        P.dma(lambda e, i=i, ct=ct: e.dma_start(out=WT[i][:], in_=wsc_d[ct]), reads=["wsc%d" % ct], writes=["wt%d" % i],
              semkey="wt%d" % i)
        return i

    def stream_w2(idx):
        i = w2c[0] % 2
        w2c[0] += 1
        P.dma(lambda e, i=i, idx=idx: e.dma_start(out=W2[i][:], in_=wsc2_d[idx]), reads=["wsc2_%d" % idx],
              writes=["w2_%d" % i], semkey="w2_%d" % i)
        return i

    def proj_fm(ct):
        wi_ = stream_ct(ct)
        bi = mm_next()
        for kt in range(8):
            P.pe(lambda e, hT=hT, bi=bi, wi_=wi_, kt=kt: e.matmul(BK[bi], lhsT=WT[wi_][:, kt, :], rhs=hT[:, kt, :],
                                                          start=(kt == 0), stop=(kt == 7)),
                 reads=["wt%d" % wi_, HT], writes=[bn(bi)])
        return bi

    def chk(n_):
        if stage == n_:
            raise _Stop()

    try:
      for b in range(nblocks):
        chk(0)
        hT = hTs[b % 2]
        HT = "hT%d" % (b % 2)
        chk(1)
        kofs = (b % 2) * TB
        for c in range(4):
            bi = proj_fm(c)
            P.act(lambda e, bi=bi, c=c: e.activation(out=qT[:, c, :], in_=BK[bi], func=AF.Copy),
                  reads=[bn(bi)], writes=["qT"])
        for c in range(4):
            bi = proj_fm(4 + c)
            P.dve(lambda e, bi=bi, c=c, kofs=kofs: e.tensor_copy(kT[:, c, kofs:kofs + TB], BK[bi]),
                  reads=[bn(bi)], writes=["kT"])
        for hf in range(2):
            wi_ = stream_w2(hf)
            for t in range(NT):
                gt = b * NT + t
                bi = mm_next()
                for kt in range(8):
                    P.pe(lambda e, hT=hT, bi=bi, wi_=wi_, kt=kt, t=t: e.matmul(BK[bi][:, 0:256], lhsT=hT[:, kt, t * 128:(t + 1) * 128],
                                                                       rhs=W2[wi_][:, kt, :], start=(kt == 0), stop=(kt == 7)),
                         reads=["w2_%d" % wi_, HT], writes=[bn(bi)])
                P.act(lambda e, bi=bi, gt=gt, hf=hf: e.activation(
                    out=Vr[:, gt % 8, 4 * hf:4 * hf + 4, 0:64], in_=BK[bi][:, 0:256].rearrange("p (h d) -> p h d", d=64),
                    func=AF.Copy), reads=[bn(bi)], writes=["Vr"])
        for hf in range(2):
            wi_ = stream_w2(2 + hf)
            for t in range(NT):
                bi = mm_next()
                for kt in range(8):
                    P.pe(lambda e, hT=hT, bi=bi, wi_=wi_, kt=kt, t=t: e.matmul(BK[bi][:, 0:256], lhsT=hT[:, kt, t * 128:(t + 1) * 128],
                                                                       rhs=W2[wi_][:, kt, :], start=(kt == 0), stop=(kt == 7)),
                         reads=["w2_%d" % wi_, HT], writes=[bn(bi)])
                P.act(lambda e, bi=bi, t=t, hf=hf: e.activation(out=zas[:, t, 256 * hf:256 * hf + 256], in_=BK[bi][:, 0:256],
                                                                 func=AF.Silu), reads=[bn(bi)], writes=["zas"])
        for c in range(4):
            bi = proj_fm(16 + c)
            P.dve(lambda e, bi=bi, c=c: e.tensor_copy(uT[:, c, :].rearrange("p (j c) -> p j c", c=NCB),
                                                      BK[bi].rearrange("p (c j) -> p j c", j=T)),
                  reads=[bn(bi)], writes=["uT%d" % c])
        for c in range(4):
            bi = proj_fm(20 + c)
            P.act(lambda e, bi=bi, c=c: e.activation(out=zss[:, c, :].rearrange("p (j c) -> p j c", c=NCB),
                                                     in_=BK[bi].rearrange("p (c j) -> p j c", j=T), func=AF.Silu),
                  reads=[bn(bi)], writes=["zss%d" % c])
        chk(2)
        def attn_scores(t, gt, dls, h, part):
            hc, po = h // 2, 64 * (h % 2)
            for d in dls:
                if (d == 4) != (part == "d4"):
                    continue
                ks = ((gt - d) % 8) * 128
                sbk, ssl = (h % 3, d) if d < 4 else (3, h % 4)
                P.pe(lambda e, sbk=sbk, ssl=ssl, po=po, hc=hc, ks=ks, t=t: e.matmul(
                    SCB[sbk][:, ssl, :], lhsT=kT[po:po + 64, hc, ks:ks + 128], rhs=qT[po:po + 64, hc, t * 128:(t + 1) * 128],
                    start=True, stop=True), reads=["kT", "qT"], writes=["sc%d" % sbk])

        def attn_exp(t, gt, dls, h):
            pset = h % 3
            pb = PB[pset]
            pres = "PB%d" % pset
            bk = SCB[h % 3]
            te = tmpE[h % 2]
            tres = "tmpE%d" % (h % 2)
            na = len([d for d in dls if d < 2])
            nb_ = len([d for d in dls if 2 <= d < 4])
            P.act(lambda e, bk=bk, na=na, te=te: e.activation(out=te[:, 0:na, :], in_=bk[:, 0:na, :], func=AF.Exp, scale=ATT_SCALE),
                  reads=["sc%d" % (h % 3)], writes=[tres])
            P.pool(lambda e, pb=pb, na=na, h=h, te=te: e.tensor_tensor(out=pb[:, 0:na, :], in0=te[:, 0:na, :], in1=EB[:, h, 0:na, :], op=ALU.mult),
                   reads=[tres, "EB"], writes=[pres + "a"])
            if nb_:
                P.act(lambda e, pb=pb, bk=bk, nb_=nb_, h=h: e.activation(out=pb[:, 2:2 + nb_, :], in_=bk[:, 2:2 + nb_, :], func=AF.Exp,
                                                                      scale=ATT_SCALE, bias=cb[:, h:h + 1]),
                      reads=["sc%d" % (h % 3), "cb"], writes=[pres + "b"])
            if 4 in dls:
                P.act(lambda e, pb=pb, h=h: e.activation(out=pb[:, 4, :], in_=SCB[3][:, h % 4, :], func=AF.Exp, scale=ATT_SCALE,
                                                       bias=cb[:, h:h + 1]), reads=["sc3", "cb"], writes=[pres + "c"])
                P.pool(lambda e, pb=pb: e.memset(pb[0:64, 4, 64:128], 0.0), reads=[pres + "c"], writes=[pres + "c"])

        def attn_pv(t, gt, dls, h):
            pb = PB[h % 3]
            pvreg = PV[h // 4][:, h % 4, 0:65]
            for n_, d in enumerate(dls):
                kt_ = gt - d
                P.pe(lambda e, pvreg=pvreg, pb=pb, d=d, kt_=kt_, h=h, n_=n_, nd=len(dls): e.matmul(
                    pvreg, lhsT=pb[:, d, :], rhs=Vr[:, kt_ % 8, h, :], start=(n_ == 0), stop=(n_ == nd - 1)),
                    reads=["PB%d%s" % (h % 3, "abc"[min(d, 4) // 2]), "Vr", "Vr_ones"], writes=["pv%d" % (h // 4)])

        def attn_prologue(t):
            gt = b * NT + t
            dls = [d for d in range(5) if gt - d >= 0]
            attn_scores(t, gt, dls, 0, "main")
            attn_scores(t, gt, dls, 0, "d4")
            attn_scores(t, gt, dls, 1, "main")
            attn_scores(t, gt, dls, 2, "main")

        def attn_heads(t, filler=None, act_hook=None):
            gt = b * NT + t
            dls = [d for d in range(5) if gt - d >= 0]
            for h in range(8):
                attn_exp(t, gt, dls, h)
                if act_hook is not None:
                    act_hook(h)
                if h + 1 < 8:
                    attn_scores(t, gt, dls, h + 1, "d4")
                attn_pv(t, gt, dls, h)
                if filler is not None:
                    filler(h)
                if h + 3 < 8:
                    attn_scores(t, gt, dls, h + 3, "main")

        def attn_tail(t):
            for hb in range(2):
                P.dve(lambda e, hb=hb: e.reciprocal(rec[:, 4 * hb:4 * hb + 4], PV[hb][:, :, 64]), reads=["pv%d" % hb],
                      writes=["rec%d" % hb])
            for h in range(8):
                P.dve(lambda e, h=h, t=t: e.scalar_tensor_tensor(
                    out=yat[:, 64 * h:64 * h + 64], in0=PV[h // 4][:, h % 4, 0:64], scalar=rec[:, h:h + 1],
                    in1=zas[:, t, 64 * h:64 * h + 64], op0=ALU.mult, op1=ALU.mult),
                    reads=["pv%d" % (h // 4), "rec%d" % (h // 4), "zas"], writes=["yat"])

        tr_bank = {}

        def attn_tr_pe(t):
            bi = mm_next()
            ring["held"].add(bi)
            tr_bank[t] = bi
            tpv = BK[bi].bitcast(BF16)[:, 0:512].rearrange("p (k c) -> p k c", c=128)
            for c in range(4):
                P.pe(lambda e, tpv=tpv, c=c: e.transpose(tpv[:, c, :], yat[:, c * 128:(c + 1) * 128], identb[:]),
                     reads=["yat", "identb"], writes=[bn(bi)])

        def attn_tr_evac(t):
            bi = tr_bank[t]
            ring["held"].discard(bi)
            tpv = BK[bi].bitcast(BF16)[:, 0:512].rearrange("p (k c) -> p k c", c=128)
            P.act(lambda e, tpv=tpv, t=t: e.activation(out=yaT[:, :, t * 128:(t + 1) * 128], in_=tpv, func=AF.Copy),
                  reads=[bn(bi)], writes=["yaT"])

        UT = ["uT%d" % c for c in range(4)]
        P.dve(lambda e: e.tensor_copy(Xh[:, :, :, 0], Xh[:, :, :, NCB]), reads=["Xh"], writes=["Xh"])
        ssm_bank = {}

        def ssm_a(q, part):
            if part == 0:
                ssm_bank[q] = mm_next()
                ring["held"].add(ssm_bank[q])
            bi = ssm_bank[q]
            Sv = BK[bi].rearrange("p (a c) -> p a c", c=NCB)
            for reim in (part // 4,):
                for kq in (part % 4,):
                    for s in range(T):
                        P.pe(lambda e, Sv=Sv, q=q, kq=kq, s=s, reim=reim: e.matmul(
                            Sv[:, 4 * reim + kq, :], lhsT=Wa[32 * q:32 * q + 32, kq, s, reim, :],
                            rhs=uT[32 * q:32 * q + 32, kq, s * NCB:(s + 1) * NCB],
                            start=(s == 0), stop=(s == T - 1), tile_position=(32 * q, 0)),
                            reads=["Wa", "uT%d" % kq], writes=[bn(bi)])

        def ssm_rec(q):
            bi = ssm_bank[q]
            ring["held"].discard(bi)
            Sv = BK[bi].rearrange("p (a c) -> p a c", c=NCB)
            for reim in range(2):
                P.dve(lambda e, Sv=Sv, reim=reim: e.tensor_copy(Sh[reim][:], Sv[:, 4 * reim:4 * reim + 4, :]),
                      reads=[bn(bi)], writes=["Sh%d" % reim])
            csv, snv = CS[:, q:16:4, :], SN[:, q:16:4, :]
            tt_("dve", Gi[0][:], Sh[0][:], csv, ALU.mult, ["Sh0"], ["Gi0"])
            tt_("dve", Gt[:], Sh[1][:], snv, ALU.mult, ["Sh1"], ["Gt"])
            tt_("dve", Gi[0][:], Gi[0][:], Gt[:], ALU.add, ["Gi0", "Gt"], ["Gi0"])
            tt_("dve", Gi[1][:], Sh[1][:], csv, ALU.mult, ["Sh1"], ["Gi1"])
            tt_("dve", Gt[:], Sh[0][:], snv, ALU.mult, ["Sh0", "Gi0"], ["Gt"])
            tt_("dve", Gi[1][:], Gi[1][:], Gt[:], ALU.subtract, ["Gi1", "Gt"], ["Gi1"])
            for reim in range(2):
                for kq in range(4):
                    p_ = 4 * kq + q
                    P.dve(lambda e, reim=reim, kq=kq, p_=p_: e.tensor_tensor_scan(
                        out=Sh[reim][:, kq, :], data0=rT[:, p_:p_ + 1].to_broadcast([128, NCB]), data1=Gi[reim][:, kq, :],
                        initial=Xl[:, reim, p_:p_ + 1], op0=ALU.mult, op1=ALU.add),
                        reads=["Gi%d" % reim, "Xl", "rT", "Sh%d" % reim], writes=["Sh%d" % reim])
            tt_("dve", Gi[0][:], Sh[0][:], csv, ALU.mult, ["Sh0"], ["Gi0"])
            tt_("dve", Gt[:], Sh[1][:], snv, ALU.mult, ["Sh1"], ["Gt"])
            tt_("dve", Gi[0][:], Gi[0][:], Gt[:], ALU.subtract, ["Gi0", "Gt"], ["Gi0"])
            tt_("dve", Gi[1][:], Sh[1][:], csv, ALU.mult, ["Sh1"], ["Gi1"])
            tt_("dve", Gt[:], Sh[0][:], snv, ALU.mult, ["Sh0", "Gi0"], ["Gt"])
            tt_("dve", Gi[1][:], Gi[1][:], Gt[:], ALU.add, ["Gi1", "Gt"], ["Gi1"])
            for reim in range(2):
                P.dve(lambda e, reim=reim, q=q: e.tensor_copy(Xh[:, reim, q:16:4, 1:NCB + 1], Gi[reim][:]),
                      reads=["Gi%d" % reim], writes=["Xh"])
                P.dve(lambda e, reim=reim, q=q: e.tensor_copy(Xl[:, reim, q:16:4], Gi[reim][:, :, NCB - 1]),
                      reads=["Gi%d" % reim], writes=["Xl"])
        nxt = b + 1 if b + 1 < nblocks else None
        if nxt is not None:
            b1_load(nxt, 0)
        def make_filler(t):
            def filler(h):
                if t + 1 < NT:
                    ssm_a(t + 1, h)
                if nxt is not None and h == 5:
                    b1_sq(nxt, t)
                if h == 3 and t >= 1:
                    attn_tr_pe(t - 1)
                if nxt is not None and h == 6 and t >= 1:
                    b1_pe(nxt, t - 1)
            return filler

        def make_hook(t):
            def hook(h):
                if h == 4 and t >= 1:
                    attn_tr_evac(t - 1)
            return hook

        ring["n"] = 2
        for part in range(8):
            ssm_a(0, part)
        ssm_rec(0)
        attn_prologue(0)
        for t in range(NT):
            attn_heads(t, filler=make_filler(t), act_hook=make_hook(t))
            if t + 1 < NT:
                attn_prologue(t + 1)
            if nxt is not None:
                b1_scale(nxt, t)
            attn_tail(t)
            if t + 1 < NT:
                ssm_rec(t + 1)
            if nxt is not None:
                if t + 1 < NT:
                    b1_load(nxt, t + 1)
        if nxt is not None:
            b1_pe(nxt, NT - 1)
        ring["n"] = 8
        chk(3)
        for kq in range(4):
            bi = mm_next()
            Y3 = BK[bi].rearrange("p (c j) -> p c j", j=T)
            U3 = uT[:, kq, :].rearrange("p (c j) -> p c j", j=T)
            for tau in range(T):
                P.pe(lambda e, Y3=Y3, U3=U3, tau=tau, kq=kq, bi=bi: e.matmul(
                    BK[bi][:, tau * NCB:TB], lhsT=Kin[:, kq, tau, :],
                    rhs=uT[:, kq, 0:(T - tau) * NCB], start=(tau == 0), stop=False),
                    reads=["Kin", "uT%d" % kq], writes=[bn(bi)])
            for q in range(4):
                p_ = 4 * kq + q
                for j in range(T):
                    for reim in range(2):
                        last = (j == T - 1 and reim == 1)
                        P.pe(lambda e, bi=bi, q=q, j=j, reim=reim, p_=p_, last=last: e.matmul(
                            BK[bi][32 * q:32 * q + 32, j * NCB:(j + 1) * NCB], lhsT=Wi[:, p_, j, reim, :], rhs=Xh[:, reim, p_, 0:NCB],
                            start=False, stop=last, tile_position=(0, 32 * q)),
                            reads=["Wi", "Xh"], writes=[bn(bi)])
            Yp = BK[bi]
            P.act(lambda e, Yp=Yp: e.activation(out=E0[:], in_=Yp, func=AF.Square), reads=[bn(bi)], writes=["E0"])
            P.dve(lambda e: e.tensor_scalar(out=E0[:], in0=E0[:], scalar1=0.044715, scalar2=1.0, op0=ALU.mult, op1=ALU.add),
                  reads=["E0"], writes=["E0"])
            tt_("dve", E0[:], E0[:], Yp, ALU.mult, ["E0", bn(bi)], ["E0"])
            P.act(lambda e: e.activation(out=E1[:], in_=E0[:], func=AF.Sigmoid, scale=1.5957691216057308), reads=["E0"], writes=["E1"])
            tt_("dve", uT[:, kq, :], E1[:], Yp, ALU.mult, ["E1", bn(bi)], ["uT%d" % kq])
        attn_tr_pe(NT - 1)
        attn_tr_evac(NT - 1)
        chk(4)
        for mt in range(4):
            bb_ = mm_next()
            for c in range(4):
                P.pe(lambda e, bb_=bb_, c=c, mt=mt: e.matmul(BK[bb_], lhsT=Wglu[:, c, 512 + mt * 128:512 + (mt + 1) * 128],
                                                            rhs=uT[:, c, :], start=(c == 0), stop=(c == 3)),
                     reads=["Wglu"] + UT, writes=[bn(bb_)])
            P.act(lambda e, bb_=bb_, mt=mt: e.activation(out=E1[:], in_=BK[bb_], func=AF.Sigmoid, bias=bgT[:, 4 + mt:5 + mt]),
                  reads=[bn(bb_), "bgT"], writes=["E1"])
            ba = mm_next()
            for c in range(4):
                P.pe(lambda e, ba=ba, c=c, mt=mt: e.matmul(BK[ba], lhsT=Wglu[:, c, mt * 128:(mt + 1) * 128], rhs=uT[:, c, :],
                                                          start=(c == 0), stop=(c == 3)), reads=["Wglu"] + UT, writes=[bn(ba)])
            P.dve(lambda e, ba=ba, mt=mt: e.scalar_tensor_tensor(out=E0[:], in0=BK[ba], scalar=bgT[:, mt:mt + 1], in1=E1[:],
                                                                 op0=ALU.add, op1=ALU.mult), reads=[bn(ba), "E1", "bgT"], writes=["E0"])
            tt_("pool", zss[:, mt, :], E0[:], zss[:, mt, :], ALU.mult, ["E0", "zss%d" % mt], ["zss%d" % mt])
        ZS = ["zss%d" % c for c in range(4)]
        chk(5)
        for dt_ in range(8):
            bga = proj_fm(24 + dt_)
            P.act(lambda e, bga=bga, dt_=dt_: e.activation(out=SG[0][:], in_=BK[bga], func=AF.Sigmoid, bias=gbT[:, dt_:dt_ + 1]),
                  reads=[bn(bga), "gbT"], writes=["SG0"])
            bgs = proj_fm(32 + dt_)
            P.act(lambda e, bgs=bgs, dt_=dt_: e.activation(out=SG[1][:], in_=BK[bgs], func=AF.Sigmoid, bias=gbT[:, 8 + dt_:9 + dt_]),
                  reads=[bn(bgs), "gbT"], writes=["SG1"])
            bya = mm_next()
            for c in range(4):
                P.pe(lambda e, bya=bya, c=c, dt_=dt_: e.matmul(BK[bya], lhsT=Wao[:, c, dt_ * 128:(dt_ + 1) * 128], rhs=yaT[:, c, :],
                                                              start=(c == 0), stop=(c == 3)), reads=["Wao", "yaT"], writes=[bn(bya)])
            tt_("dve", E0[:], BK[bya], SG[0][:], ALU.mult, [bn(bya), "SG0"], ["E0"])
            bys = mm_next()
            for c in range(4):
                P.pe(lambda e, bys=bys, c=c, dt_=dt_: e.matmul(BK[bys], lhsT=Wso[:, c, dt_ * 128:(dt_ + 1) * 128], rhs=zss[:, c, :],
                                                              start=(c == 0), stop=(c == 3)), reads=["Wso"] + ZS, writes=[bn(bys)])
            tt_("dve", E1[:, :].rearrange("p (c j) -> p c j", j=T), BK[bys].rearrange("p (j c) -> p c j", c=NCB),
                SG[1][:, :].rearrange("p (c j) -> p c j", j=T), ALU.mult, [bn(bys), "SG1"], ["E1"])
            tt_("pool", mrg[:, dt_, :], E0[:], E1[:], ALU.add, ["E0", "E1"], ["mrg"])
        chk(6)
        for t in range(NT):
            gt = b * NT + t
            xi = load_x(gt)
            for hf in range(2):
                bi = mm_next()
                for kt in range(8):
                    P.pe(lambda e, bi=bi, kt=kt, t=t, hf=hf: e.matmul(BK[bi], lhsT=mrg[:, kt, t * 128:(t + 1) * 128],
                                                                     rhs=Wout[:, kt, hf * 512:(hf + 1) * 512], start=(kt == 0), stop=(kt == 7)),
                         reads=["mrg", "Wout"], writes=[bn(bi)])
                tt_("dve", xbuf[xi][:, hf * 512:(hf + 1) * 512], BK[bi], xbuf[xi][:, hf * 512:(hf + 1) * 512], ALU.add,
                    [bn(bi), "xbuf%d" % xi], ["xbuf%d" % xi])
            P.act(lambda e, xi=xi: e.activation(out=xnb[0], in_=xbuf[xi][:], func=AF.Square, accum_out=ssq[:, 4:5]),
                  reads=["xbuf%d" % xi], writes=["E0", "ssq4"])
            P.pool(lambda e: e.tensor_scalar(out=rstd[:, 4:5], in0=ssq[:, 4:5], scalar1=1.0 / D, scalar2=1e-6, op0=ALU.mult, op1=ALU.add),
                   reads=["ssq4"], writes=["rstd4"])
            P.pool(lambda e: e.tensor_tensor(out=rstd[:, 4:5], in0=rstd[:, 4:5], in1=mhalf[:], op=ALU.pow),
                   reads=["rstd4", "mhalf"], writes=["rstd4"])
            P.dve(lambda e, xi=xi: e.scalar_tensor_tensor(out=xbuf[xi][:], in0=xbuf[xi][:], scalar=rstd[:, 4:5], in1=fgb[:],
                                                          op0=ALU.mult, op1=ALU.mult), reads=["xbuf%d" % xi, "rstd4", "fgb"],
                  writes=["xbuf%d" % xi])
            P.dma(lambda e, xi=xi, gt=gt: e.dma_start(out=out_d[gt * 128:(gt + 1) * 128, :], in_=xbuf[xi][:]),
                  reads=["xbuf%d" % xi], writes=["out"], semkey="xbo%d" % xi, eng="act")
    except _Stop:
        pass
    if maxops is not None:
        P.ops = P.ops[:maxops]
        P.barriers = [x for x in P.barriers if x <= maxops]
    stats = P.build()
    NCg.__exit__(None, None, None)
    es.close()
    return nc, stats


def host_layouts(sh):
    f = np.float32
    vecs = np.zeros((128, 92), f)
    vecs[:, 0:8] = sh["norm_gain"].reshape(8, 128).T
    vecs[:, 8:24] = sh["gate_bias"].reshape(16, 128).T
    vecs[:, 24:32] = sh["b_glu"].reshape(8, 128).T
    vecs[:, 32:36] = sh["ssm_d"].reshape(4, 128).T
    vecs[:, 36:52] = sh["ssm_a_re"].reshape(16, 2, 64).transpose(1, 2, 0).reshape(128, 16)
    vecs[:, 52:68] = sh["ssm_a_im"].reshape(16, 2, 64).transpose(1, 2, 0).reshape(128, 16)
    vecs[:, 68:84] = np.repeat(sh["ssm_log_dt"].reshape(16, 2).T[:, None, :], 64, axis=1).reshape(128, 16)
    vecs[:, 84:92] = np.repeat(sh["rel_bias"][:, 191][None, :], 128, axis=0)
    bT = np.zeros((128, 2, 16, 32), f)
    cT = np.zeros((128, 2, 16, 32), f)
    for ri, (bk, ck) in enumerate((("ssm_b_re", "ssm_c_re"), ("ssm_b_im", "ssm_c_im"))):
        bb = sh[bk].reshape(16, 2, 64, 16)
        cc = sh[ck].reshape(16, 2, 16, 64)
        for g in range(2):
            bT[64 * g:64 * g + 64, ri, :, 16 * g:16 * g + 16] = bb[:, g].transpose(1, 0, 2)
            cT[64 * g:64 * g + 64, ri, :, 16 * g:16 * g + 16] = cc[:, g].transpose(2, 0, 1)
    return {"p_vecs": vecs, "p_bT": bT, "p_cT": cT}


def kernel(**inputs):
    n = 8
    nc, stats = build_program()
    ident = np.eye(128, dtype=np.float32)
    idx = np.arange(128) // 16
    maskbd = (idx[:, None] == idx[None, :]).astype(np.float32)
    x = np.asarray(inputs["x"], dtype=np.float32)
    shared = {}
    for k, v in inputs.items():
        if k == "x":
            continue
        a = np.asarray(v, dtype=np.float32)
        if k != "final_gain":
            a = a[0]
        shared[k] = np.ascontiguousarray(a)
    shared["c_ident"] = ident
    shared["c_maskbd"] = maskbd
    rb = shared["rel_bias"]
    shared["rel_bias_x"] = np.ascontiguousarray(np.concatenate([rb, np.repeat(rb[:, 191:192], 192, axis=1)], axis=1))
    shared["c_flip"] = np.ascontiguousarray(ident[::-1])
    shared.update(host_layouts(shared))
    in_maps = []
    for c in range(n):
        m = dict(shared)
        m["x"] = np.ascontiguousarray(x[c])
        in_maps.append(m)
    res = run_bass_kernel_spmd(nc, in_maps, core_ids=list(range(n)))
    out = np.stack([np.asarray(r["out"], dtype=np.float32) for r in res.results], axis=0)
    return out
```

```python
import contextlib
import math
import numpy as np
import concourse.bass as bass
import concourse.mybir as mybir
from concourse.bass_utils import run_bass_kernel_spmd

F32 = mybir.dt.float32
BF16 = mybir.dt.bfloat16
I32 = mybir.dt.int32
AF = mybir.ActivationFunctionType
ALU = mybir.AluOpType

L = 4096
D = 1024
TB = 512
NB = L // TB
NT = TB // 128
T = 8
NCB = TB // T
ATT_SCALE = 1.0 / 8.0
ENGS = ("pe", "act", "dve", "pool", "sp")


class Prog:
    def __init__(self, nc):
        self.nc = nc
        self.ops = []
        self.barriers = []
        self.sucnt = {}

    def op(self, eng, fn, reads=(), writes=(), dma=False, semkey=None, nobar=False):
        self.ops.append(dict(eng=eng, fn=fn, reads=tuple(reads), writes=tuple(writes), dma=dma, semkey=semkey, nobar=nobar))
        return len(self.ops) - 1

    def pe(self, fn, reads=(), writes=()):
        return self.op("pe", fn, reads, writes)

    def act(self, fn, reads=(), writes=()):
        return self.op("act", fn, reads, writes)

    def dve(self, fn, reads=(), writes=()):
        return self.op("dve", fn, reads, writes)

    def pool(self, fn, reads=(), writes=()):
        return self.op("pool", fn, reads, writes)

    def dma(self, fn, reads=(), writes=(), semkey=None, eng="sp", nobar=False):
        if not (semkey.startswith("xb") or semkey.startswith("wt") or semkey.startswith("w2_")):
            c = self.sucnt.get(eng, 0)
            self.sucnt[eng] = c + 1
            semkey = "%s_su%d" % (eng, c % 4)
        return self.op(eng, fn, reads, writes, dma=True, semkey=semkey, nobar=nobar)

    def barrier(self):
        self.barriers.append(len(self.ops))

    def build(self):
        nc = self.nc
        ops = self.ops
        n = len(ops)
        last_writer = {}
        readers = {}
        deps = [set() for _ in range(n)]
        for i, o in enumerate(ops):
            for r in o["reads"]:
                if r in last_writer:
                    deps[i].add(last_writer[r])
            for w in o["writes"]:
                if w in last_writer:
                    deps[i].add(last_writer[w])
                for rd in readers.get(w, ()):
                    deps[i].add(rd)
            for r in o["reads"]:
                readers.setdefault(r, []).append(i)
            for w in o["writes"]:
                last_writer[w] = i
                readers[w] = []
            deps[i].discard(i)
        last_dma = {}
        for i, o in enumerate(ops):
            if o["dma"]:
                if o["semkey"] in last_dma:
                    deps[i].add(last_dma[o["semkey"]])
                last_dma[o["semkey"]] = i
        for bidx in self.barriers:
            pre_last = {}
            pre_dmas = []
            for i in range(bidx):
                o = ops[i]
                if o["dma"]:
                    if not o["nobar"]:
                        pre_dmas.append(i)
                else:
                    pre_last[o["eng"]] = i
            seen = set()
            for i in range(bidx, n):
                e = ops[i]["eng"]
                if e in seen:
                    continue
                seen.add(e)
                for v in pre_last.values():
                    deps[i].add(v)
                for v in pre_dmas:
                    deps[i].add(v)
                if len(seen) == len(ENGS):
                    break
        for i, o in enumerate(ops):
            rm = set()
            for d in deps[i]:
                p = ops[d]
                if p["dma"]:
                    continue
                if p["eng"] == o["eng"] and o["eng"] == "pe" and not o["dma"]:
                    rm.add(d)
            deps[i] -= rm
        needed = set()
        for i in range(n):
            needed |= deps[i]
        es = contextlib.ExitStack()
        CAP = 16000
        sems = {}
        gen = {}
        cnt = {}
        tick = {}

        def next_tick(base, inc):
            g = gen.get(base, 0)
            c = cnt.get((base, g), 0)
            if c + inc > CAP:
                g += 1
                gen[base] = g
                c = 0
            c += inc
            cnt[(base, g)] = c
            key = (base, g)
            if key not in sems:
                sems[key] = es.enter_context(nc.semaphore("s%d" % len(sems)))
            return key, c

        for i, o in enumerate(ops):
            if o["dma"]:
                tick[i] = next_tick(("dma", o["semkey"]), 16)
            elif i in needed:
                tick[i] = next_tick(("eng", o["eng"]), 1)
        per_eng = {e: [] for e in ENGS}
        for i, o in enumerate(ops):
            per_eng[o["eng"]].append(i)
        engobj = {"pe": "tensor", "act": "scalar", "dve": "vector", "pool": "gpsimd", "sp": "sync"}
        final_ticks = {}
        for i, o in enumerate(ops):
            if o["dma"]:
                final_ticks[tick[i][0]] = tick[i][1]
        with nc.Block() as block:
            for e in ENGS:
                idxs = per_eng[e]

                def body(eng, idxs=idxs, e=e):
                    waited = {}
                    for i in idxs:
                        o = ops[i]
                        for d in sorted(deps[i]):
                            k, v = tick[d]
                            if waited.get(k, 0) >= v:
                                continue
                            eng.wait_ge(sems[k], v)
                            waited[k] = v
                        ins = o["fn"](eng)
                        if i in tick:
                            k, v = tick[i]
                            ins.then_inc(sems[k], 16 if o["dma"] else 1)
                    if e == "sp":
                        for k, v in final_ticks.items():
                            if waited.get(k, 0) < v:
                                eng.wait_ge(sems[k], v)

                getattr(block, engobj[e])(body)
        es.close()
        return {e: len(per_eng[e]) for e in ENGS}


class _Stop(Exception):
    pass


def build_program(nblocks=NB, stage=99, maxops=None, marks=None):
    nc = bass.Bass("TRN2", target_bir_lowering=False)
    P = Prog(nc)

    def din(name, shape, dt=F32):
        return nc.dram_tensor(name, list(shape), dt, kind="ExternalInput")

    x_t = din("x", [L, D])
    ng_t = din("norm_gain", [D])
    win_t = din("w_in", [D, 5120])
    rb_t = din("rel_bias", [8, 192])
    are_t = din("ssm_a_re", [32, 64])
    aim_t = din("ssm_a_im", [32, 64])
    ldt_t = din("ssm_log_dt", [32])
    bre_t = din("ssm_b_re", [32, 64, 16])
    bim_t = din("ssm_b_im", [32, 64, 16])
    cre_t = din("ssm_c_re", [32, 16, 64])
    cim_t = din("ssm_c_im", [32, 16, 64])
    sd_t = din("ssm_d", [512])
    wglu_t = din("w_glu", [512, 1024])
    bglu_t = din("b_glu", [1024])
    wao_t = din("w_attn_out", [512, 1024])
    wso_t = din("w_ssm_out", [512, 1024])
    gb_t = din("gate_bias", [2048])
    wout_t = din("w_out", [D, D])
    fg_t = din("final_gain", [D])
    ident_t = din("c_ident", [128, 128])
    mask_t = din("c_maskbd", [128, 128])
    flip_t = din("c_flip", [128, 128])
    vecs_t = din("p_vecs", [128, 92])
    bT_t = din("p_bT", [128, 2, 16, 32])
    cT_t = din("p_cT", [128, 2, 16, 32])
    out_t = nc.dram_tensor("out", [L, D], F32, kind="ExternalOutput")
    wsc_t = nc.dram_tensor("wsc", [40, 128, 8, 128], BF16, kind="Internal")
    wsc2_t = nc.dram_tensor("wsc2", [4, 128, 8, 256], BF16, kind="Internal")
    rbx_t = din("rel_bias_x", [8, 384])
    x_d, out_d, win_d = x_t.ap(), out_t.ap(), win_t.ap()
    wsc_d, wsc2_d, rbx_d = wsc_t.ap(), wsc2_t.ap(), rbx_t.ap()

    es = contextlib.ExitStack()

    def sb(name, shape, dt=F32, stack=None):
        return (stack or es).enter_context(nc.sbuf_tensor(name, list(shape), dt))

    def psum(name, shape, dt=F32):
        return es.enter_context(nc.psum_tensor(name, list(shape), dt))

    NCg = nc.allow_non_contiguous_dma(reason="small setup loads")
    NCg.__enter__()

    Wglu = sb("Wglu", [128, 4, 1024], BF16)
    Wao = sb("Wao", [128, 4, 1024], BF16)
    Wso = sb("Wso", [128, 4, 1024], BF16)
    Wout = sb("Wout", [128, 8, 1024], BF16)
    Wa = sb("Wa", [128, 4, T, 2, 128], BF16)
    Wi = sb("Wi", [128, 16, T, 2, 32], BF16)
    Kin = sb("Kin", [128, 4, T, 128], BF16)
    CS = sb("CS", [128, 16, NCB])
    SN = sb("SN", [128, 16, NCB])
    rT = sb("rT", [128, 16])
    Xl = sb("Xl", [128, 2, 16])
    Xh = sb("Xh", [128, 2, 16, NCB + 1], BF16)
    identb = sb("identb", [128, 128], BF16)
    vecs = sb("vecs", [128, 92])
    g1T, gbT, bgT, cb = vecs[:, 0:8], vecs[:, 8:24], vecs[:, 24:32], vecs[:, 84:92]
    fgb = sb("fgb", [128, 1024])
    EB = sb("EB", [128, 8, 2, 128], BF16)
    mhalf = sb("mhalf", [128, 1])
    PB = [sb("PB%d" % s, [128, 5, 128], BF16) for s in range(3)]

    MM = [psum("mm%d" % i, [128, 512]) for i in range(2)]
    SCB = [psum("sc%d" % i, [128, 4, 128]) for i in range(4)]
    PV = [psum("pv%d" % i, [128, 4, 128]) for i in range(2)]
    BK = [MM[0][:, :], MM[1][:, :]] + [SCB[i][:, :, :].rearrange("p a b -> p (a b)") for i in range(4)] + \
         [PV[i][:, :, :].rearrange("p a b -> p (a b)") for i in range(2)]
    BKN = ["mm0", "mm1", "sc0", "sc1", "sc2", "sc3", "pv0", "pv1"]
    ring = {"n": 8, "c": 0, "held": set()}

    def mm_next():
        for _ in range(16):
            i = ring["c"] % ring["n"]
            ring["c"] += 1
            if i not in ring["held"]:
                return i
        raise RuntimeError("no free PSUM bank in ring")

    def bn(i):
        return BKN[i]

    xbuf = [sb("xbuf%d" % i, [128, 1024]) for i in range(3)]
    hTs = [sb("hT%d" % i, [128, 8, TB], BF16) for i in range(2)]
    E0 = sb("E0", [128, 512]); E1 = sb("E1", [128, 512])
    ssq = sb("ssq", [128, 8]); rstd = sb("rstd", [128, 8]); rec = sb("rec", [128, 8])
    xnb = [E0[:, :].bitcast(BF16), E1[:, :].bitcast(BF16)]
    xnres = ["E0", "E1"]
    b1x = {}
    ss = contextlib.ExitStack()
    identf = sb("identf", [128, 128], F32, ss)
    maskbd = sb("maskbd", [128, 128], F32, ss)
    P.dma(lambda e: e.dma_start(out=identf[:], in_=ident_t.ap()), writes=["identf"], semkey="c0")
    P.dma(lambda e: e.dma_start(out=maskbd[:], in_=mask_t.ap()), writes=["maskbd"], semkey="c1")
    P.dve(lambda e: e.tensor_copy(identb[:], identf[:]), reads=["identf"], writes=["identb"])
    P.dve(lambda e: e.memset(mhalf[:], -0.5), writes=["mhalf"])
    P.dma(lambda e: e.dma_start(out=vecs[:], in_=vecs_t.ap()),
          writes=["g1T", "gbT", "bgT", "cb", "lre", "lim", "ldt0", "ldt1", "dT"], semkey="c2")
    P.dma(lambda e: e.dma_start(out=fgb[:], in_=fg_t.ap().rearrange("(o d) -> o d", o=1).partition_broadcast(128)),
          writes=["fgb"], semkey="c5")
    xc = [0]

    def load_x(gt):
        i = xc[0] % 3
        xc[0] += 1
        P.dma(lambda e, i=i, gt=gt: e.dma_start(out=xbuf[i][:], in_=x_d[gt * 128:(gt + 1) * 128, :]),
              writes=["xbuf%d" % i], semkey="xb%d" % i)
        return i

    bTx = sb("bTx", [128, 2, 16, 32], F32, ss); cTx = sb("cTx", [128, 2, 16, 32], F32, ss)
    P.dma(lambda e: e.dma_start(out=bTx[:], in_=bT_t.ap()), writes=["bTr", "bTr_0", "bTr_1", "bTi", "bTi_0", "bTi_1"], semkey="s3")
    P.dma(lambda e: e.dma_start(out=cTx[:], in_=cT_t.ap()), writes=["cblkrT", "cblkiT"], semkey="s4")
    for t in range(3):
        b1x[(0, t)] = load_x(t)
    if marks is not None:
        marks['a_consts'] = len(P.ops)
    win_v = win_d.rearrange("(kt p) c -> p kt c", p=128)
    ct_order = list(range(0, 8)) + ["v", "za"] + list(range(16, 40))

    def cast_ct(ct):
        P.dma(lambda e, ct=ct: e.dma_start(out=wsc_d[ct], in_=win_v[:, :, ct * 128:(ct + 1) * 128]),
              writes=["wsc%d" % ct], semkey="wc%d" % (ct % 4), eng="pool", nobar=True)

    def cast_w2(idx):
        c0 = 1024 + idx * 256
        P.dma(lambda e, idx=idx, c0=c0: e.dma_start(out=wsc2_d[idx], in_=win_v[:, :, c0:c0 + 256]),
              writes=["wsc2_%d" % idx], semkey="w2c%d" % (idx % 2), eng="pool", nobar=True)

    for ct in ct_order:
        if ct == "v":
            cast_w2(0), cast_w2(1)
        elif ct == "za":
            cast_w2(2), cast_w2(3)
        else:
            cast_ct(ct)
    if marks is not None:
        marks['b_wsc'] = len(P.ops)
    for nm, wt_, src, nk in (("Wglu", Wglu, wglu_t, 4), ("Wao", Wao, wao_t, 4), ("Wso", Wso, wso_t, 4), ("Wout", Wout, wout_t, 8)):
        sv = src.ap().rearrange("(kt p) c -> p kt c", p=128)
        for kt in range(nk):
            P.dma(lambda e, wt_=wt_, sv=sv, kt=kt: e.dma_start(out=wt_[:, kt, :], in_=sv[:, kt, :]),
                  writes=[nm], semkey="wr%d" % (kt % 2), eng="pool", nobar=True)

    if marks is not None:
        marks['c_resw'] = len(P.ops)
    ebraw = sb("ebraw", [128, 8, 2, 128], F32, ss)
    flipf = sb("flipf", [128, 128], F32, ss)
    P.dma(lambda e: e.dma_start(out=flipf[:], in_=flip_t.ap()), writes=["flipf"], semkey="c9")
    P.dve(lambda e: e.memset(ebraw[:], 0.0), writes=["ebraw"])
    for h in range(8):
        base = h * 384
        P.dma(lambda e, h=h, base=base: e.dma_start(out=ebraw[:, h, 1, :], in_=bass.AP(rbx_t, base + 64, [[1, 128], [1, 128]])),
              reads=["ebraw"], writes=["ebraw%d_1" % h], semkey="eb%d" % (h % 4))
        P.dma(lambda e, h=h, base=base: e.dma_start(out=ebraw[64:128, h, 0, :], in_=bass.AP(rbx_t, base, [[1, 64], [1, 128]])),
              reads=["ebraw"], writes=["ebraw%d_0a" % h], semkey="eba%d" % (h % 4))
        P.dma(lambda e, h=h, base=base: e.dma_start(out=ebraw[0:64, h, 0, 64:128], in_=bass.AP(rbx_t, base, [[1, 64], [1, 64]])),
              reads=["ebraw"], writes=["ebraw%d_0b" % h], semkey="ebb%d" % (h % 4))
    allraw = [w for o in P.ops for w in o["writes"] if w.startswith("ebraw")]
    if marks is not None:
        marks['d_bias'] = len(P.ops)
    def st(name, shape, dt=F32):
        return sb(name, shape, dt, ss)

    lre, lim, ldt, dT = vecs[:, 36:52], vecs[:, 52:68], vecs[:, 68:84], vecs[:, 32:36]
    bTr, bTi, cTr, cTi = bTx[:, 0], bTx[:, 1], cTx[:, 0], cTx[:, 1]
    if marks is not None:
        marks['e_ssmloads'] = len(P.ops)
    RCT = ["cblkrT", "cblkiT"]
    cTrb = st("cTrb", [128, 16, 32], BF16); ncTib = st("ncTib", [128, 16, 32], BF16)
    P.dve(lambda e: e.tensor_copy(cTrb[:], cTr[:]), reads=RCT, writes=["cTrb"])
    P.dve(lambda e: e.tensor_scalar(out=ncTib[:], in0=cTi[:], scalar1=-1.0, scalar2=None, op0=ALU.mult),
          reads=RCT, writes=["ncTib"])

    def tt_(eng, out, a, b, op, reads, writes):
        getattr(P, eng)(lambda e: e.tensor_tensor(out=out, in0=a, in1=b, op=op), reads=reads, writes=writes)

    if marks is not None:
        marks['f_ctrans'] = len(P.ops)
    names = ["dtv", "th", "lr", "mag", "kq", "kf", "thr", "thc", "sn", "cs", "abr", "abi", "nr", "den", "rden",
             "fr", "fi", "t1", "t2", "t3"]
    V = {nm: st("v_" + nm, [128, 16]) for nm in names}
    ki = st("v_ki", [128, 16], I32)
    SA = ["lre", "lim", "ldt0", "ldt1"]
    P.act(lambda e: e.activation(out=V["dtv"][:], in_=ldt[:], func=AF.Exp), reads=SA, writes=["dtv"])
    tt_("dve", V["th"][:], lim[:], V["dtv"][:], ALU.mult, SA + ["dtv"], ["th"])
    tt_("dve", V["lr"][:], lre[:], V["dtv"][:], ALU.mult, SA + ["dtv"], ["lr"])
    P.act(lambda e: e.activation(out=V["mag"][:], in_=V["lr"][:], func=AF.Exp), reads=["lr"], writes=["mag"])
    P.act(lambda e: e.activation(out=rT[:], in_=V["lr"][:], func=AF.Exp, scale=float(T)), reads=["lr"], writes=["rT"])
    TWO_PI = 2.0 * math.pi

    def range_reduce(dst, src_name, shift):
        P.dve(lambda e: e.tensor_scalar(out=V["kq"][:], in0=V[src_name][:], scalar1=shift, scalar2=1.0 / TWO_PI,
                                        op0=ALU.add, op1=ALU.mult), reads=[src_name], writes=["kq"])
        P.dve(lambda e: e.tensor_copy(ki[:], V["kq"][:]), reads=["kq"], writes=["ki"])
        P.dve(lambda e: e.tensor_copy(V["kf"][:], ki[:]), reads=["ki"], writes=["kf"])
        P.dve(lambda e: e.tensor_scalar(out=V["kq"][:], in0=V[src_name][:], scalar1=shift, scalar2=None, op0=ALU.add),
              reads=[src_name, "kf"], writes=["kq"])
        P.dve(lambda e: e.scalar_tensor_tensor(out=V[dst][:], in0=V["kf"][:], scalar=-TWO_PI, in1=V["kq"][:],
                                               op0=ALU.mult, op1=ALU.add), reads=["kf", "kq"], writes=[dst])

    range_reduce("thr", "th", 0.0)
    P.act(lambda e: e.activation(out=V["sn"][:], in_=V["thr"][:], func=AF.Sin), reads=["thr"], writes=["sn"])
    range_reduce("thc", "th", math.pi / 2.0)
    P.act(lambda e: e.activation(out=V["cs"][:], in_=V["thc"][:], func=AF.Sin), reads=["thc"], writes=["cs"])
    tt_("dve", V["abr"][:], V["mag"][:], V["cs"][:], ALU.mult, ["mag", "cs"], ["abr"])
    tt_("dve", V["abi"][:], V["mag"][:], V["sn"][:], ALU.mult, ["mag", "sn"], ["abi"])
    P.dve(lambda e: e.tensor_scalar(out=V["nr"][:], in0=V["abr"][:], scalar1=-1.0, scalar2=None, op0=ALU.add),
          reads=["abr"], writes=["nr"])
    tt_("dve", V["t1"][:], lre[:], lre[:], ALU.mult, SA, ["t1"])
    tt_("dve", V["t2"][:], lim[:], lim[:], ALU.mult, SA, ["t2"])
    tt_("dve", V["den"][:], V["t1"][:], V["t2"][:], ALU.add, ["t1", "t2"], ["den"])
    P.dve(lambda e: e.reciprocal(V["rden"][:], V["den"][:]), reads=["den"], writes=["rden"])
    tt_("dve", V["t1"][:], V["nr"][:], lre[:], ALU.mult, ["nr"] + SA, ["t1"])
    tt_("dve", V["t2"][:], V["abi"][:], lim[:], ALU.mult, ["abi"] + SA, ["t2"])
    tt_("dve", V["t3"][:], V["t1"][:], V["t2"][:], ALU.add, ["t1", "t2"], ["t3"])
    tt_("dve", V["fr"][:], V["t3"][:], V["rden"][:], ALU.mult, ["t3", "rden"], ["fr"])
    tt_("dve", V["t1"][:], V["abi"][:], lre[:], ALU.mult, ["abi"] + SA, ["t1"])
    tt_("dve", V["t2"][:], V["nr"][:], lim[:], ALU.mult, ["nr"] + SA, ["t2"])
    tt_("dve", V["t3"][:], V["t1"][:], V["t2"][:], ALU.subtract, ["t1", "t2"], ["t3"])
    tt_("dve", V["fi"][:], V["t3"][:], V["rden"][:], ALU.mult, ["t3", "rden"], ["fi"])
    if marks is not None:
        marks['g_scal'] = len(P.ops)
    def b1_load(bb, t):
        b1x[(bb, t)] = load_x(bb * NT + t)

    def b1_act(bb, t):
        b1_sq(bb, t)
        b1_scale(bb, t)

    def b1_sq(bb, t):
        xi = b1x[(bb, t)]
        xw, xres = xnb[t % 2], xnres[t % 2]
        P.act(lambda e, xi=xi, t=t, xw=xw: e.activation(out=xw, in_=xbuf[xi][:], func=AF.Square, accum_out=ssq[:, t:t + 1]),
              reads=["xbuf%d" % xi], writes=[xres, "ssq%d" % t])
        P.pool(lambda e, t=t: e.tensor_scalar(out=rstd[:, t:t + 1], in0=ssq[:, t:t + 1], scalar1=1.0 / D, scalar2=1e-6,
                                              op0=ALU.mult, op1=ALU.add), reads=["ssq%d" % t], writes=["rstd%d" % t])
        P.pool(lambda e, t=t: e.tensor_tensor(out=rstd[:, t:t + 1], in0=rstd[:, t:t + 1], in1=mhalf[:], op=ALU.pow),
               reads=["rstd%d" % t, "mhalf"], writes=["rstd%d" % t])

    def b1_scale(bb, t):
        xi = b1x[(bb, t)]
        xw, xres = xnb[t % 2], xnres[t % 2]
        P.act(lambda e, xi=xi, t=t, xw=xw: e.activation(out=xw, in_=xbuf[xi][:], func=AF.Copy, scale=rstd[:, t:t + 1]),
              reads=["xbuf%d" % xi, "rstd%d" % t], writes=[xres])

    def b1_pe(bb, t):
        hTw = hTs[bb % 2]
        hres = "hT%d" % (bb % 2)
        xw, xres = xnb[t % 2], xnres[t % 2]
        bi = mm_next()
        tpv = BK[bi].bitcast(BF16).rearrange("p (k c) -> p k c", c=128)
        for kt in range(8):
            P.pe(lambda e, tpv=tpv, kt=kt, xw=xw: e.transpose(tpv[:, kt, :], xw[:, kt * 128:(kt + 1) * 128], identb[:]),
                 reads=[xres, "identb"], writes=[bn(bi)])
        P.dve(lambda e, tpv=tpv, t=t, hTw=hTw: e.tensor_tensor(out=hTw[:, :, t * 128:(t + 1) * 128], in0=tpv,
                                                              in1=g1T[:, :].unsqueeze(2).to_broadcast([128, 8, 128]), op=ALU.mult),
              reads=[bn(bi), "g1T"], writes=[hres])


    b0_banks = {}

    def b1_front0(t):
        if (0, t) not in b1x:
            b1_load(0, t)
        xi = b1x[(0, t)]
        xw, xres = xnb[t % 2], xnres[t % 2]
        P.act(lambda e, xi=xi, t=t, xw=xw: e.activation(out=xw, in_=xbuf[xi][:], func=AF.Square, accum_out=ssq[:, t:t + 1]),
              reads=["xbuf%d" % xi], writes=[xres, "ssq%d" % t])
        P.act(lambda e, t=t: e.activation(out=rstd[:, t:t + 1], in_=ssq[:, t:t + 1], func=AF.Ln, scale=1.0 / D, bias=1e-6),
              reads=["ssq%d" % t], writes=["rstd%d" % t])
        P.act(lambda e, t=t: e.activation(out=rstd[:, t:t + 1], in_=rstd[:, t:t + 1], func=AF.Exp, scale=-0.5),
              reads=["rstd%d" % t], writes=["rstd%d" % t])
        b1_scale(0, t)
        bi = mm_next()
        ring["held"].add(bi)
        b0_banks[t] = bi
        tpv = BK[bi].bitcast(BF16).rearrange("p (k c) -> p k c", c=128)
        for kt in range(8):
            P.pe(lambda e, tpv=tpv, kt=kt, xw=xw: e.transpose(tpv[:, kt, :], xw[:, kt * 128:(kt + 1) * 128], identb[:]),
                 reads=[xres, "identb"], writes=[bn(bi)])

    def b1_back0(t):
        bi = b0_banks[t]
        ring["held"].discard(bi)
        tpv = BK[bi].bitcast(BF16).rearrange("p (k c) -> p k c", c=128)
        P.dve(lambda e, tpv=tpv, t=t: e.tensor_tensor(out=hTs[0][:, :, t * 128:(t + 1) * 128], in0=tpv,
                                                      in1=g1T[:, :].unsqueeze(2).to_broadcast([128, 8, 128]), op=ALU.mult),
              reads=[bn(bi), "g1T"], writes=["hT0"])

    for t in range(NT):
        b1_front0(t)
    Pr = st("Pr", [128, T + 1, 16]); Pi = st("Pi", [128, T + 1, 16])
    P.dve(lambda e: e.memset(Pr[:, 0, :], 1.0), writes=["Pr0"])
    P.dve(lambda e: e.memset(Pi[:, 0, :], 0.0), writes=["Pi0"])
    for k in range(T):
        a, b_ = "Pr%d" % k, "Pi%d" % k
        tt_("dve", V["t1"][:], Pr[:, k, :], V["abr"][:], ALU.mult, [a, "abr"], ["t1"])
        tt_("dve", V["t2"][:], Pi[:, k, :], V["abi"][:], ALU.mult, [b_, "abi"], ["t2"])
        tt_("dve", Pr[:, k + 1, :], V["t1"][:], V["t2"][:], ALU.subtract, ["t1", "t2"], ["Pr%d" % (k + 1)])
        tt_("dve", V["t1"][:], Pr[:, k, :], V["abi"][:], ALU.mult, [a, "abi"], ["t1"])
        tt_("dve", V["t2"][:], Pi[:, k, :], V["abr"][:], ALU.mult, [b_, "abr"], ["t2"])
        tt_("dve", Pi[:, k + 1, :], V["t1"][:], V["t2"][:], ALU.add, ["t1", "t2"], ["Pi%d" % (k + 1)])
    P.dve(lambda e: e.reciprocal(V["t3"][:], rT[:]), reads=["rT"], writes=["t3"])
    tt_("dve", CS[:, :, 0], Pr[:, T, :], V["t3"][:], ALU.mult, ["Pr%d" % T, "t3"], ["CS"])
    tt_("dve", SN[:, :, 0], Pi[:, T, :], V["t3"][:], ALU.mult, ["Pi%d" % T, "t3"], ["SN"])
    tb1 = st("tb1", [128, 16, 32]); tb2 = st("tb2", [128, 16, 32])
    w = 1
    while w < NCB:
        cbr = CS[:, :, w - 1:w].to_broadcast([128, 16, w])
        sbr = SN[:, :, w - 1:w].to_broadcast([128, 16, w])
        src_c, src_s = CS[:, :, 0:w], SN[:, :, 0:w]
        t1v, t2v = tb1[:, :, 0:w], tb2[:, :, 0:w]
        tt_("dve", t1v, src_c, cbr, ALU.mult, ["CS", "SN"], ["tb1"])
        tt_("dve", t2v, src_s, sbr, ALU.mult, ["CS", "SN"], ["tb2"])
        tt_("dve", t1v, t1v, t2v, ALU.subtract, ["tb1", "tb2"], ["tb1"])
        tt_("dve", t2v, src_c, sbr, ALU.mult, ["CS", "SN", "tb1"], ["tb2"])
        tt_("dve", CS[:, :, w:2 * w], t1v, t1v, ALU.max, ["tb1", "tb2"], ["CS"])
        tt_("dve", t1v, src_s, cbr, ALU.mult, ["CS", "SN"], ["tb1"])
        tt_("dve", SN[:, :, w:2 * w], t1v, t2v, ALU.add, ["tb1", "tb2"], ["SN"])
        w *= 2
    if marks is not None:
        marks['h_tables'] = len(P.ops)
    bbr = st("bbr", [128, 16, 32]); bbi = st("bbi", [128, 16, 32])
    u1 = st("u1", [128, 16, 32]); u2 = st("u2", [128, 16, 32])
    BR = ["bTr", "bTr_0", "bTr_1", "bTi", "bTi_0", "bTi_1"]

    def bc(v_):
        return v_.unsqueeze(2).to_broadcast([128, 16, 32])

    tt_("dve", u1[:], bTr[:], bc(V["fr"][:]), ALU.mult, BR + ["fr"], ["u1"])
    tt_("dve", u2[:], bTi[:], bc(V["fi"][:]), ALU.mult, BR + ["fi"], ["u2"])
    tt_("dve", bbr[:], u1[:], u2[:], ALU.subtract, ["u1", "u2"], ["bbr"])
    tt_("dve", u1[:], bTi[:], bc(V["fr"][:]), ALU.mult, BR + ["fr"], ["u1"])
    tt_("dve", u2[:], bTr[:], bc(V["fi"][:]), ALU.mult, BR + ["fi"], ["u2"])
    tt_("dve", bbi[:], u1[:], u2[:], ALU.add, ["u1", "u2"], ["bbi"])
    BpTr = st("BpTr", [128, T, 16, 32], BF16); BpTi = st("BpTi", [128, T, 16, 32], BF16)
    for k in range(T):
        pr, pi = bc(Pr[:, k, :]), bc(Pi[:, k, :])
        tt_("dve", u1[:], bbr[:], pr, ALU.mult, ["bbr", "Pr%d" % k], ["u1"])
        tt_("dve", u2[:], bbi[:], pi, ALU.mult, ["bbi", "Pi%d" % k], ["u2"])
        tt_("dve", BpTr[:, k, :, :], u1[:], u2[:], ALU.subtract, ["u1", "u2"], ["BpTr%d" % k])
        tt_("dve", u1[:], bbr[:], pi, ALU.mult, ["bbr", "Pi%d" % k], ["u1"])
        tt_("dve", u2[:], bbi[:], pr, ALU.mult, ["bbi", "Pr%d" % k], ["u2"])
        tt_("dve", BpTi[:, k, :, :], u1[:], u2[:], ALU.add, ["u1", "u2"], ["BpTi%d" % k])
    if marks is not None:
        marks['i_bpt'] = len(P.ops)
    for kq in range(4):
        for reim, src in ((0, BpTr), (1, BpTi)):
            bi = mm_next()
            tpv = BK[bi].bitcast(BF16)[:, 0:T * 128].rearrange("p (s c) -> p s c", c=128)
            for s in range(T):
                k = T - 1 - s
                P.pe(lambda e, tpv=tpv, s=s, src=src, k=k, kq=kq: e.transpose(
                    tpv[:, s, :], src[:, k, 4 * kq:4 * kq + 4, :].rearrange("p a b -> p (a b)"), identb[:]),
                    reads=["BpT%s%d" % ("ri"[reim], k), "identb"], writes=[bn(bi)])
            P.act(lambda e, tpv=tpv, kq=kq, reim=reim: e.activation(out=Wa[:, kq, :, reim, :], in_=tpv, func=AF.Copy),
                  reads=[bn(bi)], writes=["Wa"])
    if marks is not None:
        marks['j_wa'] = len(P.ops)
    for j in range(T):
        pr, pi = bc(Pr[:, j + 1, :]), bc(Pi[:, j + 1, :])
        rd = RCT + ["Pr%d" % (j + 1), "Pi%d" % (j + 1)]
        tt_("dve", u1[:], cTr[:], pr, ALU.mult, rd, ["u1"])
        tt_("dve", u2[:], cTi[:], pi, ALU.mult, rd, ["u2"])
        tt_("dve", Wi[:, :, j, 0, :], u1[:], u2[:], ALU.subtract, ["u1", "u2"], ["Wi"])
        tt_("dve", u1[:], cTr[:], pi, ALU.mult, rd, ["u1"])
        tt_("dve", u2[:], cTi[:], pr, ALU.mult, rd, ["u2"])
        P.dve(lambda e, j=j: e.scalar_tensor_tensor(out=Wi[:, :, j, 1, :], in0=u1[:], scalar=-1.0, in1=u2[:],
                                                    op0=ALU.mult, op1=ALU.subtract), reads=["u1", "u2"], writes=["Wi"])
    if marks is not None:
        marks['k_wi'] = len(P.ops)
    kt1 = st("kt1", [128, 128])
    for kq in range(4):
        for tau in range(T):
            bi = mm_next()
            ov = BK[bi][:, 0:128]
            sl = slice(4 * kq, 4 * kq + 4)
            P.pe(lambda e, ov=ov, tau=tau, sl=sl: e.matmul(
                ov, lhsT=BpTr[:, tau, sl, :].rearrange("p a b -> p (a b)"),
                rhs=cTrb[:, sl, :].rearrange("p a b -> p (a b)"), start=True, stop=False),
                reads=["BpTr%d" % tau, "cTrb"], writes=[bn(bi)])
            P.pe(lambda e, ov=ov, tau=tau, sl=sl: e.matmul(
                ov, lhsT=BpTi[:, tau, sl, :].rearrange("p a b -> p (a b)"),
                rhs=ncTib[:, sl, :].rearrange("p a b -> p (a b)"), start=False, stop=True),
                reads=["BpTi%d" % tau, "ncTib"], writes=[bn(bi)])
            if tau == 0:
                tt_("dve", kt1[:], ov, maskbd[:], ALU.mult, [bn(bi), "maskbd"], ["kt1"])
                P.dve(lambda e, kq=kq: e.scalar_tensor_tensor(out=Kin[:, kq, 0, :], in0=identf[:], scalar=dT[:, kq:kq + 1],
                                                             in1=kt1[:], op0=ALU.mult, op1=ALU.add),
                      reads=["kt1", "identf", "dT"], writes=["Kin"])
            else:
                tt_("dve", Kin[:, kq, tau, :], ov, maskbd[:], ALU.mult, [bn(bi), "maskbd"], ["Kin"])
    for h0 in range(0, 8, 2):
        bi = mm_next()
        for k_ in range(4):
            hh, dd = h0 + k_ // 2, k_ % 2
            P.pe(lambda e, bi=bi, k_=k_, hh=hh, dd=dd: e.matmul(BK[bi][:, k_ * 128:(k_ + 1) * 128], lhsT=flipf[:], rhs=ebraw[:, hh, dd, :],
                                                              start=True, stop=True), reads=allraw + ["flipf"], writes=[bn(bi)])
        P.act(lambda e, bi=bi, h0=h0: e.activation(out=EB[:, h0:h0 + 2, :, :].rearrange("p a b c -> p (a b c)"), in_=BK[bi],
                                                   func=AF.Exp), reads=[bn(bi)], writes=["EB"])
    P.dve(lambda e: e.memset(EB[64:128, :, 0, 0:64], 0.0), reads=["EB"], writes=["EB"])

    P.dve(lambda e: e.memset(Xl[:], 0.0), writes=["Xl"])
    P.dve(lambda e: e.memset(Xh[:], 0.0), writes=["Xh"])
    if marks is not None:
        marks['l_kin'] = len(P.ops)
    for t in range(NT):
        b1_back0(t)
    ss.close()
    P.barrier()

    qT = sb("qT", [128, 4, TB], BF16)
    kT = sb("kT", [128, 4, 2 * TB], BF16)
    Vr = sb("Vr", [128, 8, 8, 65], BF16)
    zas = sb("zas", [128, NT, 512], BF16)
    uT = sb("uT", [128, 4, TB], BF16)
    zss = sb("zss", [128, 4, TB], BF16)
    yaT = sb("yaT", [128, 4, TB], BF16)
    yat = sb("yat", [128, 512], BF16)
    mrg = sb("mrg", [128, 8, TB], BF16)
    WT = [sb("wt%d" % i, [128, 8, 128], BF16) for i in range(4)]
    W2 = [sb("w2_%d" % i, [128, 8, 256], BF16) for i in range(2)]
    tmpE = [sb("tmpE%d" % i, [128, 2, 128]) for i in range(2)]
    Sh = [sb("Sh%d" % i, [128, 4, NCB]) for i in range(2)]
    Gi = [sb("Gi%d" % i, [128, 4, NCB]) for i in range(2)]
    Gt = sb("Gt", [128, 4, NCB])
    SG = [sb("SG%d" % i, [128, 512], BF16) for i in range(2)]
    P.dve(lambda e: e.memset(Vr[:, :, :, 64:65], 1.0), writes=["Vr_ones"])

    wtc = [0]
    w2c = [0]
    scc = [0]

    def stream_ct(ct):
        i = wtc[0] % 4
        wtc[0] += 1
        P.dma(lambda e, i=i, ct=ct: e.dma_start(out=WT[i][:], in_=wsc_d[ct]), reads=["wsc%d" % ct], writes=["wt%d" % i],
              semkey="wt%d" % i)
        return i

    def stream_w2(idx):
        i = w2c[0] % 2
        w2c[0] += 1
        P.dma(lambda e, i=i, idx=idx: e.dma_start(out=W2[i][:], in_=wsc2_d[idx]), reads=["wsc2_%d" % idx],
              writes=["w2_%d" % i], semkey="w2_%d" % i)
        return i

    def proj_fm(ct):
        wi_ = stream_ct(ct)
        bi = mm_next()
        for kt in range(8):
            P.pe(lambda e, hT=hT, bi=bi, wi_=wi_, kt=kt: e.matmul(BK[bi], lhsT=WT[wi_][:, kt, :], rhs=hT[:, kt, :],
                                                          start=(kt == 0), stop=(kt == 7)),
                 reads=["wt%d" % wi_, HT], writes=[bn(bi)])
        return bi

    def chk(n_):
        if stage == n_:
            raise _Stop()

    try:
      for b in range(nblocks):
        chk(0)
        hT = hTs[b % 2]
        HT = "hT%d" % (b % 2)
        chk(1)
        kofs = (b % 2) * TB
        for c in range(4):
            bi = proj_fm(c)
            P.act(lambda e, bi=bi, c=c: e.activation(out=qT[:, c, :], in_=BK[bi], func=AF.Copy),
                  reads=[bn(bi)], writes=["qT"])
        for c in range(4):
            bi = proj_fm(4 + c)
            P.dve(lambda e, bi=bi, c=c, kofs=kofs: e.tensor_copy(kT[:, c, kofs:kofs + TB], BK[bi]),
                  reads=[bn(bi)], writes=["kT"])
        for hf in range(2):
            wi_ = stream_w2(hf)
            for t in range(NT):
                gt = b * NT + t
                bi = mm_next()
                for kt in range(8):
                    P.pe(lambda e, hT=hT, bi=bi, wi_=wi_, kt=kt, t=t: e.matmul(BK[bi][:, 0:256], lhsT=hT[:, kt, t * 128:(t + 1) * 128],
                                                                       rhs=W2[wi_][:, kt, :], start=(kt == 0), stop=(kt == 7)),
                         reads=["w2_%d" % wi_, HT], writes=[bn(bi)])
                P.act(lambda e, bi=bi, gt=gt, hf=hf: e.activation(
                    out=Vr[:, gt % 8, 4 * hf:4 * hf + 4, 0:64], in_=BK[bi][:, 0:256].rearrange("p (h d) -> p h d", d=64),
                    func=AF.Copy), reads=[bn(bi)], writes=["Vr"])
        for hf in range(2):
            wi_ = stream_w2(2 + hf)
            for t in range(NT):
                bi = mm_next()
                for kt in range(8):
                    P.pe(lambda e, hT=hT, bi=bi, wi_=wi_, kt=kt, t=t: e.matmul(BK[bi][:, 0:256], lhsT=hT[:, kt, t * 128:(t + 1) * 128],
                                                                       rhs=W2[wi_][:, kt, :], start=(kt == 0), stop=(kt == 7)),
                         reads=["w2_%d" % wi_, HT], writes=[bn(bi)])
                P.act(lambda e, bi=bi, t=t, hf=hf: e.activation(out=zas[:, t, 256 * hf:256 * hf + 256], in_=BK[bi][:, 0:256],
                                                                 func=AF.Silu), reads=[bn(bi)], writes=["zas"])
        for c in range(4):
            bi = proj_fm(16 + c)
            P.dve(lambda e, bi=bi, c=c: e.tensor_copy(uT[:, c, :].rearrange("p (j c) -> p j c", c=NCB),
                                                      BK[bi].rearrange("p (c j) -> p j c", j=T)),
                  reads=[bn(bi)], writes=["uT%d" % c])
        for c in range(4):
            bi = proj_fm(20 + c)
            P.act(lambda e, bi=bi, c=c: e.activation(out=zss[:, c, :].rearrange("p (j c) -> p j c", c=NCB),
                                                     in_=BK[bi].rearrange("p (c j) -> p j c", j=T), func=AF.Silu),
                  reads=[bn(bi)], writes=["zss%d" % c])
        chk(2)
        def attn_scores(t, gt, dls, h, part):
            hc, po = h // 2, 64 * (h % 2)
            for d in dls:
                if (d == 4) != (part == "d4"):
                    continue
                ks = ((gt - d) % 8) * 128
                sbk, ssl = (h % 3, d) if d < 4 else (3, h % 4)
                P.pe(lambda e, sbk=sbk, ssl=ssl, po=po, hc=hc, ks=ks, t=t: e.matmul(
                    SCB[sbk][:, ssl, :], lhsT=kT[po:po + 64, hc, ks:ks + 128], rhs=qT[po:po + 64, hc, t * 128:(t + 1) * 128],
                    start=True, stop=True), reads=["kT", "qT"], writes=["sc%d" % sbk])

        def attn_exp(t, gt, dls, h):
            pset = h % 3
            pb = PB[pset]
            pres = "PB%d" % pset
            bk = SCB[h % 3]
            te = tmpE[h % 2]
            tres = "tmpE%d" % (h % 2)
            na = len([d for d in dls if d < 2])
            nb_ = len([d for d in dls if 2 <= d < 4])
            P.act(lambda e, bk=bk, na=na, te=te: e.activation(out=te[:, 0:na, :], in_=bk[:, 0:na, :], func=AF.Exp, scale=ATT_SCALE),
                  reads=["sc%d" % (h % 3)], writes=[tres])
            P.pool(lambda e, pb=pb, na=na, h=h, te=te: e.tensor_tensor(out=pb[:, 0:na, :], in0=te[:, 0:na, :], in1=EB[:, h, 0:na, :], op=ALU.mult),
                   reads=[tres, "EB"], writes=[pres + "a"])
            if nb_:
                P.act(lambda e, pb=pb, bk=bk, nb_=nb_, h=h: e.activation(out=pb[:, 2:2 + nb_, :], in_=bk[:, 2:2 + nb_, :], func=AF.Exp,
                                                                      scale=ATT_SCALE, bias=cb[:, h:h + 1]),
                      reads=["sc%d" % (h % 3), "cb"], writes=[pres + "b"])
            if 4 in dls:
                P.act(lambda e, pb=pb, h=h: e.activation(out=pb[:, 4, :], in_=SCB[3][:, h % 4, :], func=AF.Exp, scale=ATT_SCALE,
                                                       bias=cb[:, h:h + 1]), reads=["sc3", "cb"], writes=[pres + "c"])
                P.pool(lambda e, pb=pb: e.memset(pb[0:64, 4, 64:128], 0.0), reads=[pres + "c"], writes=[pres + "c"])

        def attn_pv(t, gt, dls, h):
            pb = PB[h % 3]
            pvreg = PV[h // 4][:, h % 4, 0:65]
            for n_, d in enumerate(dls):
                kt_ = gt - d
                P.pe(lambda e, pvreg=pvreg, pb=pb, d=d, kt_=kt_, h=h, n_=n_, nd=len(dls): e.matmul(
                    pvreg, lhsT=pb[:, d, :], rhs=Vr[:, kt_ % 8, h, :], start=(n_ == 0), stop=(n_ == nd - 1)),
                    reads=["PB%d%s" % (h % 3, "abc"[min(d, 4) // 2]), "Vr", "Vr_ones"], writes=["pv%d" % (h // 4)])

        def attn_prologue(t):
            gt = b * NT + t
            dls = [d for d in range(5) if gt - d >= 0]
            attn_scores(t, gt, dls, 0, "main")
            attn_scores(t, gt, dls, 0, "d4")
            attn_scores(t, gt, dls, 1, "main")
            attn_scores(t, gt, dls, 2, "main")

        def attn_heads(t, filler=None, act_hook=None):
            gt = b * NT + t
            dls = [d for d in range(5) if gt - d >= 0]
            for h in range(8):
                attn_exp(t, gt, dls, h)
                if act_hook is not None:
                    act_hook(h)
                if h + 1 < 8:
                    attn_scores(t, gt, dls, h + 1, "d4")
                attn_pv(t, gt, dls, h)
                if filler is not None:
                    filler(h)
                if h + 3 < 8:
                    attn_scores(t, gt, dls, h + 3, "main")

        def attn_tail(t):
            for hb in range(2):
                P.dve(lambda e, hb=hb: e.reciprocal(rec[:, 4 * hb:4 * hb + 4], PV[hb][:, :, 64]), reads=["pv%d" % hb],
                      writes=["rec%d" % hb])
            for h in range(8):
                P.dve(lambda e, h=h, t=t: e.scalar_tensor_tensor(
                    out=yat[:, 64 * h:64 * h + 64], in0=PV[h // 4][:, h % 4, 0:64], scalar=rec[:, h:h + 1],
                    in1=zas[:, t, 64 * h:64 * h + 64], op0=ALU.mult, op1=ALU.mult),
                    reads=["pv%d" % (h // 4), "rec%d" % (h // 4), "zas"], writes=["yat"])

        tr_bank = {}

        def attn_tr_pe(t):
            bi = mm_next()
            ring["held"].add(bi)
            tr_bank[t] = bi
            tpv = BK[bi].bitcast(BF16)[:, 0:512].rearrange("p (k c) -> p k c", c=128)
            for c in range(4):
                P.pe(lambda e, tpv=tpv, c=c: e.transpose(tpv[:, c, :], yat[:, c * 128:(c + 1) * 128], identb[:]),
                     reads=["yat", "identb"], writes=[bn(bi)])

        def attn_tr_evac(t):
            bi = tr_bank[t]
            ring["held"].discard(bi)
            tpv = BK[bi].bitcast(BF16)[:, 0:512].rearrange("p (k c) -> p k c", c=128)
            P.act(lambda e, tpv=tpv, t=t: e.activation(out=yaT[:, :, t * 128:(t + 1) * 128], in_=tpv, func=AF.Copy),
                  reads=[bn(bi)], writes=["yaT"])

        UT = ["uT%d" % c for c in range(4)]
        P.dve(lambda e: e.tensor_copy(Xh[:, :, :, 0], Xh[:, :, :, NCB]), reads=["Xh"], writes=["Xh"])
        ssm_bank = {}

        def ssm_a(q, part):
            if part == 0:
                ssm_bank[q] = mm_next()
                ring["held"].add(ssm_bank[q])
            bi = ssm_bank[q]
            Sv = BK[bi].rearrange("p (a c) -> p a c", c=NCB)
            for reim in (part // 4,):
                for kq in (part % 4,):
                    for s in range(T):
                        P.pe(lambda e, Sv=Sv, q=q, kq=kq, s=s, reim=reim: e.matmul(
                            Sv[:, 4 * reim + kq, :], lhsT=Wa[32 * q:32 * q + 32, kq, s, reim, :],
                            rhs=uT[32 * q:32 * q + 32, kq, s * NCB:(s + 1) * NCB],
                            start=(s == 0), stop=(s == T - 1), tile_position=(32 * q, 0)),
                            reads=["Wa", "uT%d" % kq], writes=[bn(bi)])

        def ssm_rec(q):
            bi = ssm_bank[q]
            ring["held"].discard(bi)
            Sv = BK[bi].rearrange("p (a c) -> p a c", c=NCB)
            for reim in range(2):
                P.dve(lambda e, Sv=Sv, reim=reim: e.tensor_copy(Sh[reim][:], Sv[:, 4 * reim:4 * reim + 4, :]),
                      reads=[bn(bi)], writes=["Sh%d" % reim])
            csv, snv = CS[:, q:16:4, :], SN[:, q:16:4, :]
            tt_("dve", Gi[0][:], Sh[0][:], csv, ALU.mult, ["Sh0"], ["Gi0"])
            tt_("dve", Gt[:], Sh[1][:], snv, ALU.mult, ["Sh1"], ["Gt"])
            tt_("dve", Gi[0][:], Gi[0][:], Gt[:], ALU.add, ["Gi0", "Gt"], ["Gi0"])
            tt_("dve", Gi[1][:], Sh[1][:], csv, ALU.mult, ["Sh1"], ["Gi1"])
            tt_("dve", Gt[:], Sh[0][:], snv, ALU.mult, ["Sh0", "Gi0"], ["Gt"])
            tt_("dve", Gi[1][:], Gi[1][:], Gt[:], ALU.subtract, ["Gi1", "Gt"], ["Gi1"])
            for reim in range(2):
                for kq in range(4):
                    p_ = 4 * kq + q
                    P.dve(lambda e, reim=reim, kq=kq, p_=p_: e.tensor_tensor_scan(
                        out=Sh[reim][:, kq, :], data0=rT[:, p_:p_ + 1].to_broadcast([128, NCB]), data1=Gi[reim][:, kq, :],
                        initial=Xl[:, reim, p_:p_ + 1], op0=ALU.mult, op1=ALU.add),
                        reads=["Gi%d" % reim, "Xl", "rT", "Sh%d" % reim], writes=["Sh%d" % reim])
            tt_("dve", Gi[0][:], Sh[0][:], csv, ALU.mult, ["Sh0"], ["Gi0"])
            tt_("dve", Gt[:], Sh[1][:], snv, ALU.mult, ["Sh1"], ["Gt"])
            tt_("dve", Gi[0][:], Gi[0][:], Gt[:], ALU.subtract, ["Gi0", "Gt"], ["Gi0"])
            tt_("dve", Gi[1][:], Sh[1][:], csv, ALU.mult, ["Sh1"], ["Gi1"])
            tt_("dve", Gt[:], Sh[0][:], snv, ALU.mult, ["Sh0", "Gi0"], ["Gt"])
            tt_("dve", Gi[1][:], Gi[1][:], Gt[:], ALU.add, ["Gi1", "Gt"], ["Gi1"])
            for reim in range(2):
                P.dve(lambda e, reim=reim, q=q: e.tensor_copy(Xh[:, reim, q:16:4, 1:NCB + 1], Gi[reim][:]),
                      reads=["Gi%d" % reim], writes=["Xh"])
                P.dve(lambda e, reim=reim, q=q: e.tensor_copy(Xl[:, reim, q:16:4], Gi[reim][:, :, NCB - 1]),
                      reads=["Gi%d" % reim], writes=["Xl"])
        nxt = b + 1 if b + 1 < nblocks else None
        if nxt is not None:
            b1_load(nxt, 0)
        def make_filler(t):
            def filler(h):
                if t + 1 < NT:
                    ssm_a(t + 1, h)
                if nxt is not None and h == 5:
                    b1_sq(nxt, t)
                if h == 3 and t >= 1:
                    attn_tr_pe(t - 1)
                if nxt is not None and h == 6 and t >= 1:
                    b1_pe(nxt, t - 1)
            return filler

        def make_hook(t):
            def hook(h):
                if h == 4 and t >= 1:
                    attn_tr_evac(t - 1)
            return hook

        ring["n"] = 2
        for part in range(8):
            ssm_a(0, part)
        ssm_rec(0)
        attn_prologue(0)
        for t in range(NT):
            attn_heads(t, filler=make_filler(t), act_hook=make_hook(t))
            if t + 1 < NT:
                attn_prologue(t + 1)
            if nxt is not None:
                b1_scale(nxt, t)
            attn_tail(t)
            if t + 1 < NT:
                ssm_rec(t + 1)
            if nxt is not None:
                if t + 1 < NT:
                    b1_load(nxt, t + 1)
        if nxt is not None:
            b1_pe(nxt, NT - 1)
        ring["n"] = 8
        chk(3)
        for kq in range(4):
            bi = mm_next()
            Y3 = BK[bi].rearrange("p (c j) -> p c j", j=T)
            U3 = uT[:, kq, :].rearrange("p (c j) -> p c j", j=T)
            for tau in range(T):
                P.pe(lambda e, Y3=Y3, U3=U3, tau=tau, kq=kq, bi=bi: e.matmul(
                    BK[bi][:, tau * NCB:TB], lhsT=Kin[:, kq, tau, :],
                    rhs=uT[:, kq, 0:(T - tau) * NCB], start=(tau == 0), stop=False),
                    reads=["Kin", "uT%d" % kq], writes=[bn(bi)])
            for q in range(4):
                p_ = 4 * kq + q
                for j in range(T):
                    for reim in range(2):
                        last = (j == T - 1 and reim == 1)
                        P.pe(lambda e, bi=bi, q=q, j=j, reim=reim, p_=p_, last=last: e.matmul(
                            BK[bi][32 * q:32 * q + 32, j * NCB:(j + 1) * NCB], lhsT=Wi[:, p_, j, reim, :], rhs=Xh[:, reim, p_, 0:NCB],
                            start=False, stop=last, tile_position=(0, 32 * q)),
                            reads=["Wi", "Xh"], writes=[bn(bi)])
            Yp = BK[bi]
            P.act(lambda e, Yp=Yp: e.activation(out=E0[:], in_=Yp, func=AF.Square), reads=[bn(bi)], writes=["E0"])
            P.dve(lambda e: e.tensor_scalar(out=E0[:], in0=E0[:], scalar1=0.044715, scalar2=1.0, op0=ALU.mult, op1=ALU.add),
                  reads=["E0"], writes=["E0"])
            tt_("dve", E0[:], E0[:], Yp, ALU.mult, ["E0", bn(bi)], ["E0"])
            P.act(lambda e: e.activation(out=E1[:], in_=E0[:], func=AF.Sigmoid, scale=1.5957691216057308), reads=["E0"], writes=["E1"])
            tt_("dve", uT[:, kq, :], E1[:], Yp, ALU.mult, ["E1", bn(bi)], ["uT%d" % kq])
        attn_tr_pe(NT - 1)
        attn_tr_evac(NT - 1)
        chk(4)
        for mt in range(4):
            bb_ = mm_next()
            for c in range(4):
                P.pe(lambda e, bb_=bb_, c=c, mt=mt: e.matmul(BK[bb_], lhsT=Wglu[:, c, 512 + mt * 128:512 + (mt + 1) * 128],
                                                            rhs=uT[:, c, :], start=(c == 0), stop=(c == 3)),
                     reads=["Wglu"] + UT, writes=[bn(bb_)])
            P.act(lambda e, bb_=bb_, mt=mt: e.activation(out=E1[:], in_=BK[bb_], func=AF.Sigmoid, bias=bgT[:, 4 + mt:5 + mt]),
                  reads=[bn(bb_), "bgT"], writes=["E1"])
            ba = mm_next()
            for c in range(4):
                P.pe(lambda e, ba=ba, c=c, mt=mt: e.matmul(BK[ba], lhsT=Wglu[:, c, mt * 128:(mt + 1) * 128], rhs=uT[:, c, :],
                                                          start=(c == 0), stop=(c == 3)), reads=["Wglu"] + UT, writes=[bn(ba)])
            P.dve(lambda e, ba=ba, mt=mt: e.scalar_tensor_tensor(out=E0[:], in0=BK[ba], scalar=bgT[:, mt:mt + 1], in1=E1[:],
                                                                 op0=ALU.add, op1=ALU.mult), reads=[bn(ba), "E1", "bgT"], writes=["E0"])
            tt_("pool", zss[:, mt, :], E0[:], zss[:, mt, :], ALU.mult, ["E0", "zss%d" % mt], ["zss%d" % mt])
        ZS = ["zss%d" % c for c in range(4)]
        chk(5)
        for dt_ in range(8):
            bga = proj_fm(24 + dt_)
            P.act(lambda e, bga=bga, dt_=dt_: e.activation(out=SG[0][:], in_=BK[bga], func=AF.Sigmoid, bias=gbT[:, dt_:dt_ + 1]),
                  reads=[bn(bga), "gbT"], writes=["SG0"])
            bgs = proj_fm(32 + dt_)
            P.act(lambda e, bgs=bgs, dt_=dt_: e.activation(out=SG[1][:], in_=BK[bgs], func=AF.Sigmoid, bias=gbT[:, 8 + dt_:9 + dt_]),
                  reads=[bn(bgs), "gbT"], writes=["SG1"])
            bya = mm_next()
            for c in range(4):
                P.pe(lambda e, bya=bya, c=c, dt_=dt_: e.matmul(BK[bya], lhsT=Wao[:, c, dt_ * 128:(dt_ + 1) * 128], rhs=yaT[:, c, :],
                                                              start=(c == 0), stop=(c == 3)), reads=["Wao", "yaT"], writes=[bn(bya)])
            tt_("dve", E0[:], BK[bya], SG[0][:], ALU.mult, [bn(bya), "SG0"], ["E0"])
            bys = mm_next()
            for c in range(4):
                P.pe(lambda e, bys=bys, c=c, dt_=dt_: e.matmul(BK[bys], lhsT=Wso[:, c, dt_ * 128:(dt_ + 1) * 128], rhs=zss[:, c, :],
                                                              start=(c == 0), stop=(c == 3)), reads=["Wso"] + ZS, writes=[bn(bys)])
            tt_("dve", E1[:, :].rearrange("p (c j) -> p c j", j=T), BK[bys].rearrange("p (j c) -> p c j", c=NCB),
                SG[1][:, :].rearrange("p (c j) -> p c j", j=T), ALU.mult, [bn(bys), "SG1"], ["E1"])
            tt_("pool", mrg[:, dt_, :], E0[:], E1[:], ALU.add, ["E0", "E1"], ["mrg"])
        chk(6)
        for t in range(NT):
            gt = b * NT + t
            xi = load_x(gt)
            for hf in range(2):
                bi = mm_next()
                for kt in range(8):
                    P.pe(lambda e, bi=bi, kt=kt, t=t, hf=hf: e.matmul(BK[bi], lhsT=mrg[:, kt, t * 128:(t + 1) * 128],
                                                                     rhs=Wout[:, kt, hf * 512:(hf + 1) * 512], start=(kt == 0), stop=(kt == 7)),
                         reads=["mrg", "Wout"], writes=[bn(bi)])
                tt_("dve", xbuf[xi][:, hf * 512:(hf + 1) * 512], BK[bi], xbuf[xi][:, hf * 512:(hf + 1) * 512], ALU.add,
                    [bn(bi), "xbuf%d" % xi], ["xbuf%d" % xi])
            P.act(lambda e, xi=xi: e.activation(out=xnb[0], in_=xbuf[xi][:], func=AF.Square, accum_out=ssq[:, 4:5]),
                  reads=["xbuf%d" % xi], writes=["E0", "ssq4"])
            P.pool(lambda e: e.tensor_scalar(out=rstd[:, 4:5], in0=ssq[:, 4:5], scalar1=1.0 / D, scalar2=1e-6, op0=ALU.mult, op1=ALU.add),
                   reads=["ssq4"], writes=["rstd4"])
            P.pool(lambda e: e.tensor_tensor(out=rstd[:, 4:5], in0=rstd[:, 4:5], in1=mhalf[:], op=ALU.pow),
                   reads=["rstd4", "mhalf"], writes=["rstd4"])
            P.dve(lambda e, xi=xi: e.scalar_tensor_tensor(out=xbuf[xi][:], in0=xbuf[xi][:], scalar=rstd[:, 4:5], in1=fgb[:],
                                                          op0=ALU.mult, op1=ALU.mult), reads=["xbuf%d" % xi, "rstd4", "fgb"],
                  writes=["xbuf%d" % xi])
            P.dma(lambda e, xi=xi, gt=gt: e.dma_start(out=out_d[gt * 128:(gt + 1) * 128, :], in_=xbuf[xi][:]),
                  reads=["xbuf%d" % xi], writes=["out"], semkey="xbo%d" % xi, eng="act")
    except _Stop:
        pass
    if maxops is not None:
        P.ops = P.ops[:maxops]
        P.barriers = [x for x in P.barriers if x <= maxops]
    stats = P.build()
    NCg.__exit__(None, None, None)
    es.close()
    return nc, stats


def host_layouts(sh):
    f = np.float32
    vecs = np.zeros((128, 92), f)
    vecs[:, 0:8] = sh["norm_gain"].reshape(8, 128).T
    vecs[:, 8:24] = sh["gate_bias"].reshape(16, 128).T
    vecs[:, 24:32] = sh["b_glu"].reshape(8, 128).T
    vecs[:, 32:36] = sh["ssm_d"].reshape(4, 128).T
    vecs[:, 36:52] = sh["ssm_a_re"].reshape(16, 2, 64).transpose(1, 2, 0).reshape(128, 16)
    vecs[:, 52:68] = sh["ssm_a_im"].reshape(16, 2, 64).transpose(1, 2, 0).reshape(128, 16)
    vecs[:, 68:84] = np.repeat(sh["ssm_log_dt"].reshape(16, 2).T[:, None, :], 64, axis=1).reshape(128, 16)
    vecs[:, 84:92] = np.repeat(sh["rel_bias"][:, 191][None, :], 128, axis=0)
    bT = np.zeros((128, 2, 16, 32), f)
    cT = np.zeros((128, 2, 16, 32), f)
    for ri, (bk, ck) in enumerate((("ssm_b_re", "ssm_c_re"), ("ssm_b_im", "ssm_c_im"))):
        bb = sh[bk].reshape(16, 2, 64, 16)
        cc = sh[ck].reshape(16, 2, 16, 64)
        for g in range(2):
            bT[64 * g:64 * g + 64, ri, :, 16 * g:16 * g + 16] = bb[:, g].transpose(1, 0, 2)
            cT[64 * g:64 * g + 64, ri, :, 16 * g:16 * g + 16] = cc[:, g].transpose(2, 0, 1)
    return {"p_vecs": vecs, "p_bT": bT, "p_cT": cT}


def kernel(**inputs):
    n = 8
    nc, stats = build_program()
    ident = np.eye(128, dtype=np.float32)
    idx = np.arange(128) // 16
    maskbd = (idx[:, None] == idx[None, :]).astype(np.float32)
    x = np.asarray(inputs["x"], dtype=np.float32)
    shared = {}
    for k, v in inputs.items():
        if k == "x":
            continue
        a = np.asarray(v, dtype=np.float32)
        if k != "final_gain":
            a = a[0]
        shared[k] = np.ascontiguousarray(a)
    shared["c_ident"] = ident
    shared["c_maskbd"] = maskbd
    rb = shared["rel_bias"]
    shared["rel_bias_x"] = np.ascontiguousarray(np.concatenate([rb, np.repeat(rb[:, 191:192], 192, axis=1)], axis=1))
    shared["c_flip"] = np.ascontiguousarray(ident[::-1])
    shared.update(host_layouts(shared))
    in_maps = []
    for c in range(n):
        m = dict(shared)
        m["x"] = np.ascontiguousarray(x[c])
        in_maps.append(m)
    res = run_bass_kernel_spmd(nc, in_maps, core_ids=list(range(n)))
    out = np.stack([np.asarray(r["out"], dtype=np.float32) for r in res.results], axis=0)
    return out
```
